# Optimizing a Trainium2 kernel written in Bass

```python
import math
import jax, jax.numpy as jnp
from jax import lax
import numpy as np

D_MODEL = 1024
BATCH = 16
SEQ = 4096
DEPTH = 1

D_MIX = D_MODEL
MLSTM_WIDTH = D_MIX // 2
MLSTM_HEADS = 4
MLSTM_HEAD_DIM = MLSTM_WIDTH // MLSTM_HEADS
MLSTM_CHUNK = 64
QK_CONV_WIDTH = 4
CONV_WIDTH_CH = D_MIX - MLSTM_WIDTH
CONV_HEADS = 4
CONV_KERNEL = 31
D_FF = int(math.ceil((8 * D_MODEL / 3) / 256) * 256)
NORM_EPS = 1e-6
F_BIAS_INIT = 3.0

IN_SPLITS = [
    MLSTM_WIDTH,
    MLSTM_WIDTH,
    MLSTM_WIDTH,
    MLSTM_WIDTH,
    MLSTM_HEADS,
    MLSTM_HEADS,
    CONV_WIDTH_CH,
    CONV_WIDTH_CH,
]
D_IN = sum(IN_SPLITS)

kernel_name = "hymba_mlstm_conformer_conv_sandwich"


def rms_norm(x, w):
    xf = x.astype(jnp.float32)
    y = xf * lax.rsqrt(jnp.mean(xf * xf, axis=-1, keepdims=True) + NORM_EPS)
    return (y * w.astype(jnp.float32)).astype(x.dtype)


def group_layer_norm(x, groups, w, b=None):
    shp = x.shape
    xf = x.astype(jnp.float32).reshape(shp[:-1] + (groups, shp[-1] // groups))
    mu = jnp.mean(xf, axis=-1, keepdims=True)
    var = jnp.mean(jnp.square(xf - mu), axis=-1, keepdims=True)
    y = ((xf - mu) * lax.rsqrt(var + NORM_EPS)).reshape(shp) * w.astype(jnp.float32)
    if b is not None:
        y = y + b.astype(jnp.float32)
    return y.astype(x.dtype)


def causal_depthwise_conv(x, w, b):
    K, C = w.shape
    y = lax.conv_general_dilated(
        x, w[:, None, :].astype(x.dtype), window_strides=(1,), padding=((K - 1, 0),),
        dimension_numbers=("NWC", "WIO", "NWC"), feature_group_count=C)
    return y + b.astype(x.dtype)


def mlstm_chunkwise(q, k, v, i_pre, f_pre):
    B, H, S, D = q.shape
    L = MLSTM_CHUNK
    NC = S // L
    f32 = jnp.float32
    q = q.astype(f32) * (D ** -0.5)
    k = k.astype(f32)
    v = v.astype(f32)
    log_i = i_pre.astype(f32)
    log_f = jax.nn.log_sigmoid(f_pre.astype(f32))

    def chunked(t):
        return jnp.moveaxis(t.reshape((B, H, NC, L) + t.shape[3:]), 2, 0)

    causal = jnp.tril(jnp.ones((L, L), dtype=bool))

    def step(carry, inp):
        C, n, m = carry
        qc, kc, vc, ic, fc = inp
        b = jnp.cumsum(fc, axis=-1)
        dmat = jnp.where(causal, b[..., :, None] - b[..., None, :] + ic[..., None, :], -jnp.inf)
        inter = b + m[..., None]
        m_t = jnp.maximum(inter, jnp.max(dmat, axis=-1))
        w_inter = jnp.exp(inter - m_t)
        s = jnp.einsum("bhtd,bhsd->bhts", qc, kc) * jnp.exp(dmat - m_t[..., None])
        num = w_inter[..., None] * jnp.einsum("bhtd,bhde->bhte", qc, C) \
            + jnp.einsum("bhts,bhse->bhte", s, vc)
        den = w_inter * jnp.einsum("bhtd,bhd->bht", qc, n) + jnp.sum(s, axis=-1)
        h = num / jnp.maximum(jnp.abs(den), jnp.exp(-m_t))[..., None]
        b_last = b[..., -1]
        a = b_last[..., None] - b + ic
        m_new = jnp.maximum(b_last + m, jnp.max(a, axis=-1))
        decay = jnp.exp(b_last + m - m_new)
        wa = jnp.exp(a - m_new[..., None])
        C_new = decay[..., None, None] * C + jnp.einsum("bhs,bhsd,bhse->bhde", wa, kc, vc)
        n_new = decay[..., None] * n + jnp.einsum("bhs,bhsd->bhd", wa, kc)
        return (C_new, n_new, m_new), h

    init = (jnp.zeros((B, H, D, D), f32), jnp.zeros((B, H, D), f32), jnp.zeros((B, H), f32))
    _, hs = lax.scan(step, init, (chunked(q), chunked(k), chunked(v), chunked(log_i), chunked(log_f)))
    return jnp.moveaxis(hs, 0, 2).reshape(B, H, S, D)


def hybrid_mixer(h, w_in, qk_conv_w, qk_conv_b, i_bias, f_bias, mh_norm_w,
                 dw_conv_w, dw_conv_b, conv_norm_w, conv_norm_b, w_out):
    B, S, _ = h.shape
    proj = jnp.einsum("bsd,df->bsf", h, w_in.astype(h.dtype))
    offs = np.cumsum(IN_SPLITS)[:-1].tolist()
    q, k, v, o, ig, fg, cv, cg = jnp.split(proj, offs, axis=-1)

    qk = jax.nn.silu(causal_depthwise_conv(jnp.concatenate([q, k], axis=-1), qk_conv_w, qk_conv_b))
    q, k = jnp.split(qk, 2, axis=-1)

    def heads(t):
        return t.reshape(B, S, MLSTM_HEADS, MLSTM_HEAD_DIM).transpose(0, 2, 1, 3)

    i_pre = (ig + i_bias.astype(ig.dtype)).transpose(0, 2, 1)
    f_pre = (fg + f_bias.astype(fg.dtype)).transpose(0, 2, 1)
    cell = mlstm_chunkwise(heads(q), heads(k), heads(v), i_pre, f_pre)
    cell = cell.transpose(0, 2, 1, 3).reshape(B, S, MLSTM_WIDTH).astype(h.dtype)
    y_mlstm = jax.nn.sigmoid(o) * group_layer_norm(cell, MLSTM_HEADS, mh_norm_w)

    u = cv * jax.nn.sigmoid(cg)
    u = causal_depthwise_conv(u, dw_conv_w, dw_conv_b)
    y_conv = jax.nn.silu(group_layer_norm(u, CONV_HEADS, conv_norm_w, conv_norm_b))

    y = jnp.concatenate([y_mlstm, y_conv], axis=-1)
    return jnp.einsum("bsf,fd->bsd", y, w_out.astype(h.dtype))


def swiglu_ffn(h, w_gate, w_up, w_down):
    g = jnp.einsum("bsd,df->bsf", h, w_gate.astype(h.dtype))
    u = jnp.einsum("bsd,df->bsf", h, w_up.astype(h.dtype))
    return jnp.einsum("bsf,fd->bsd", jax.nn.silu(g) * u, w_down.astype(h.dtype))


def setup_inputs(seed: int = 0) -> dict:
    key = jax.random.key(seed)
    ks = jax.random.split(key, 20)
    f32 = jnp.float32

    def nrm(k, shape, scale):
        return jax.random.normal(k, shape, f32) * scale

    def gain(k, shape):
        return 1.0 + 0.05 * jax.random.normal(k, shape, f32)

    L = DEPTH
    return {
        "x": jax.random.normal(ks[0], (BATCH, SEQ, D_MODEL), f32),
        "ln_mix_pre": gain(ks[1], (L, D_MODEL)),
        "ln_mix_post": gain(ks[2], (L, D_MODEL)),
        "w_in": nrm(ks[3], (L, D_MODEL, D_IN), D_MODEL ** -0.5),
        "qk_conv_w": nrm(ks[4], (L, QK_CONV_WIDTH, 2 * MLSTM_WIDTH), QK_CONV_WIDTH ** -0.5),
        "qk_conv_b": nrm(ks[5], (L, 2 * MLSTM_WIDTH), 0.02),
        "i_bias": nrm(ks[6], (L, MLSTM_HEADS), 0.1),
        "f_bias": F_BIAS_INIT + nrm(ks[7], (L, MLSTM_HEADS), 0.1),
        "mh_norm_w": gain(ks[8], (L, MLSTM_WIDTH)),
        "dw_conv_w": nrm(ks[9], (L, CONV_KERNEL, CONV_WIDTH_CH), CONV_KERNEL ** -0.5),
        "dw_conv_b": nrm(ks[10], (L, CONV_WIDTH_CH), 0.02),
        "conv_norm_w": gain(ks[11], (L, CONV_WIDTH_CH)),
        "conv_norm_b": nrm(ks[12], (L, CONV_WIDTH_CH), 0.02),
        "w_out": nrm(ks[13], (L, D_MIX, D_MODEL), D_MIX ** -0.5),
        "ln_ffn_pre": gain(ks[14], (L, D_MODEL)),
        "ln_ffn_post": gain(ks[15], (L, D_MODEL)),
        "w_gate": nrm(ks[16], (L, D_MODEL, D_FF), D_MODEL ** -0.5),
        "w_up": nrm(ks[17], (L, D_MODEL, D_FF), D_MODEL ** -0.5),
        "w_down": nrm(ks[18], (L, D_FF, D_MODEL), D_FF ** -0.5),
    }


def reference(x, ln_mix_pre, ln_mix_post, w_in, qk_conv_w, qk_conv_b, i_bias, f_bias,
              mh_norm_w, dw_conv_w, dw_conv_b, conv_norm_w, conv_norm_b, w_out,
              ln_ffn_pre, ln_ffn_post, w_gate, w_up, w_down):
    h = x
    for l in range(DEPTH):
        mix = hybrid_mixer(rms_norm(h, ln_mix_pre[l]), w_in[l], qk_conv_w[l], qk_conv_b[l],
                           i_bias[l], f_bias[l], mh_norm_w[l], dw_conv_w[l], dw_conv_b[l],
                           conv_norm_w[l], conv_norm_b[l], w_out[l])
        h = h + rms_norm(mix, ln_mix_post[l])
        ff = swiglu_ffn(rms_norm(h, ln_ffn_pre[l]), w_gate[l], w_up[l], w_down[l])
        h = h + rms_norm(ff, ln_ffn_post[l])
    return h
```

```python
import numpy as np
from contextlib import ExitStack
import concourse.bass as bass
import concourse.mybir as mybir
from concourse.bass_utils import run_bass_kernel_spmd

F32 = mybir.dt.float32
BF16 = mybir.dt.bfloat16
AF = mybir.ActivationFunctionType
ALU = mybir.AluOpType
AX = mybir.AxisListType

D = 1024
DIN = 3080
DFF = 2816
NFB = 22
EPS = 1e-6
QSCALE = 128 ** -0.5
NCORES = 8
RUN_KW = {}


class Sched:
    CE = ("pe", "act", "dve", "pool")
    ENG = ("pe", "act", "dve", "pool", "sp")

    def __init__(self, nc, es):
        self.nc = nc
        self.es = es
        self.q = {e: [] for e in self.ENG}
        self.sem = {e: es.enter_context(nc.semaphore(f"s_{e}")) for e in self.CE}
        self.cnt = {e: 0 for e in self.CE}
        self.dsem = {}
        self.waited = {e: {} for e in self.ENG}
        self.lastw = {}
        self.readers = {}

    def _semobj(self, key):
        if key in self.sem:
            return self.sem[key]
        return self.dsem[key][0]

    def _need(self, eng, tok, waits):
        if tok is None:
            return
        key, val = tok
        if key == "pe" and eng == "pe":
            return
        if self.waited[eng].get(key, 0) >= val:
            return
        if waits.get(key, 0) < val:
            waits[key] = val

    def op(self, eng, fn, reads=(), writes=(), signal=True, dma=None):
        waits = {}
        for r in reads:
            self._need(eng, self.lastw.get(r), waits)
            if isinstance(r, tuple) and r[0] == "ps":
                for t in self.readers.get(r, {}).items():
                    if t[0] != eng:
                        self._need(eng, t, waits)
        for w in writes:
            self._need(eng, self.lastw.get(w), waits)
            for t in self.readers.get(w, {}).items():
                self._need(eng, t, waits)
        if dma is not None:
            if dma not in self.dsem:
                self.dsem[dma] = [self.es.enter_context(self.nc.semaphore(f"d_{len(self.dsem)}")), 0]
            ent = self.dsem[dma]
            ent[1] += 16
            tok = (dma, ent[1])
            sig = (ent[0], 16)
        elif signal:
            self.cnt[eng] += 1
            tok = (eng, self.cnt[eng])
            sig = (self.sem[eng], 1)
        else:
            tok = (eng, self.cnt[eng] + 1)
            sig = None
        wl = [(self._semobj(k), v) for k, v in waits.items()]
        for k, v in waits.items():
            self.waited[eng][k] = v
        self.q[eng].append((fn, wl, sig))
        for r in reads:
            d = self.readers.setdefault(r, {})
            if d.get(tok[0], 0) < tok[1]:
                d[tok[0]] = tok[1]
        for w in writes:
            self.lastw[w] = tok
            self.readers[w] = {}
        return tok

    def final_wait(self, eng):
        wl = [(ent[0], ent[1]) for k, ent in self.dsem.items()]
        self.q[eng].append((None, wl, None))

    def emit(self):
        nc = self.nc

        def run(e, lst):
            for fn, wl, sig in lst:
                for s, v in wl:
                    e.wait_ge(s, v)
                if fn is None:
                    continue
                ins = fn(e)
                if sig is not None:
                    ins.then_inc(sig[0], sig[1])

        with nc.Block() as block:
            @block.tensor
            def _(e):
                run(e, self.q["pe"])

            @block.scalar
            def _(e):
                run(e, self.q["act"])

            @block.vector
            def _(e):
                run(e, self.q["dve"])

            @block.gpsimd
            def _(e):
                run(e, self.q["pool"])

            @block.sync
            def _(e):
                run(e, self.q["sp"])


class Ring:
    def __init__(self, nslots, nitems, loadfn):
        self.n = nslots
        self.nitems = nitems
        self.loadfn = loadfn
        self.next_load = 0
        self.done_upto = 0

    def pump(self):
        while self.next_load < self.nitems and self.next_load <= self.done_upto + self.n - 1:
            self.loadfn(self.next_load, self.next_load % self.n)
            self.next_load += 1

    def need(self, i):
        self.pump()
        assert i < self.next_load, (i, self.next_load, self.done_upto)
        return i % self.n

    def done(self, i):
        assert i == self.done_upto, (i, self.done_upto)
        self.done_upto = i + 1
        self.pump()


def build(nseq, nmt):
    T = 512
    seq = nmt * T
    NG = nseq * nmt
    nc = bass.Bass("TRN2", target_bir_lowering=False)
    dt_in = lambda name, shape: nc.dram_tensor(name, shape, F32, kind="ExternalInput").ap()
    x_d = dt_in("x", [nseq, seq, D])
    w_in_d = dt_in("w_in", [1, D, DIN])[0]
    w_out_d = dt_in("w_out", [1, D, D])[0]
    w_gate_d = dt_in("w_gate", [1, D, DFF])[0]
    w_up_d = dt_in("w_up", [1, D, DFF])[0]
    w_down_d = dt_in("w_down", [1, DFF, D])[0]
    pvec_d = dt_in("pvec", [128, 196])
    grow_d = dt_in("grow", [4, 2])
    lnrep_d = dt_in("lnrep", [128, 2, D])
    cst_d = dt_in("cst", [128, 256])
    out_d = nc.dram_tensor("out", [nseq, seq, D], F32, kind="ExternalOutput").ap()
    sc_in = nc.dram_tensor("sc_in", [D, DIN], BF16).ap()
    sc_out = nc.dram_tensor("sc_out", [D, D], BF16).ap()
    sc_gate = nc.dram_tensor("sc_gate", [D, DFF], BF16).ap()
    sc_up = nc.dram_tensor("sc_up", [D, DFF], BF16).ap()
    sc_down = nc.dram_tensor("sc_down", [DFF, D], BF16).ap()

    with ExitStack() as es:
        S = Sched(nc, es)
        TL = lambda name, shape, dt: es.enter_context(nc.sbuf_tensor("sb_" + name, shape, dt))
        R = 5
        NX = 8
        wring = [TL(f"wring{i}", [128, 8, 512], BF16) for i in range(R)]
        xres = [TL(f"xres{i}", [128, D], F32) for i in range(NX)]
        xs = [TL(f"xs{i}", [128, D], BF16) for i in range(2)]
        junk = TL("junk", [128, D], BF16)
        xnT = TL("xnT", [128, 8, T], BF16)
        hnT = xnT
        qkst = [TL(f"qkst{i}", [128, T + 3], BF16) for i in range(2)]
        qhalo = TL("qhalo", [128, 8, 3], BF16)
        qkT = TL("qkT", [128, 8, T], BF16)
        vp = TL("vp", [128, 4, 4, 129], BF16)
        tho = TL("tho", [128, 4, T], BF16)
        ust = [TL(f"ust{i}", [128, T + 30], BF16) for i in range(2)]
        uhalo = TL("uhalo", [128, 4, 30], BF16)
        dg31 = [TL(f"dg31_{i}", [128, 31, 128], BF16) for i in range(2)]
        dg4 = TL("dg4", [128, 32, 128], BF16)
        NF = 6
        fpool = [TL(f"fp{i}", [128, T], F32) for i in range(NF)]
        rbm = TL("rbm", [128, 2, 4, 128], F32)
        yT = TL("yT", [128, 8, T], BF16)
        Cf = TL("Cf", [128, 4, 129], F32)
        Cdb = TL("Cdb", [128, 4, 129], BF16)
        stb = [TL(f"stb{i}", [128, 128], BF16) for i in range(4)]
        ktok = [TL(f"ktok{i}", [128, 128], BF16) for i in range(4)]
        hn = TL("hn", [128, 4, 128], F32)
        ytok = [TL(f"ytok{i}", [128, T], BF16) for i in range(2)]
        g_e = TL("g_e", [4, T], F32)
        g_cs = TL("g_cs", [4, T], F32)
        g_gp = TL("g_gp", [4, T], F32)
        lnrep = TL("lnrep", [128, 2, D], F32)
        actT = TL("actT", [128, NFB, T], BF16)
        identf = TL("identf", [128, 128], F32)
        identb = TL("identb", [128, 128], BF16)
        maskS = TL("maskS", [128, 128], F32)
        pv = TL("pv", [128, 196], F32)
        sm = TL("sm", [128, 192], F32)
        grow = TL("grow", [4, 2], F32)
        gsm = TL("gsm", [4, 64], F32)
        wgi = TL("wgi", [128, 8, 4], BF16)
        wgf = TL("wgf", [128, 8, 4], BF16)
        ps = [es.enter_context(nc.psum_tensor(f"ps{i}", [128, 512], F32)) for i in range(8)]
        psbf = [p[:].bitcast(BF16) for p in ps]

        ssx = sm[:, 0:4]; msx = sm[:, 4:8]; rstdx = sm[:, 8:12]
        mhalf = sm[:, 12:28]
        onescol = sm[:, 28:29]
        mhw5 = sm[:, 29:33]
        gtok = sm[:, 33:65]
        decb = sm[:, 65:81]
        cst8 = sm[:, 81:89]
        c4a = sm[:, 89:93]; c4b = sm[:, 93:97]; c4c = sm[:, 97:101]; c4d = sm[:, 101:105]
        absd = sm[:, 105:109]; ddv = sm[:, 109:113]; rec = sm[:, 113:117]
        bst = sm[:, 117:141]
        mv = sm[:, 141:149]
        e4a = sm[:, 149:153]; e4b = sm[:, 153:157]
        decbq = sm[:, 165:181]
        nfb = gsm[:, 0:1]; carryB = gsm[:, 1:2]; cm = gsm[:, 2:6]; Rt = gsm[:, 6:11]; negR = gsm[:, 11:15]
        dR = gsm[:, 15:19]; dec = gsm[:, 19:23]; negbig = gsm[:, 23:27]; Dg = gsm[:, 27:43]
        ones4 = TL("ones4", [4, 128], F32)

        pe = lambda fn, r, w, sig=True: S.op("pe", fn, r, w, signal=sig)
        act = lambda fn, r, w: S.op("act", fn, r, w)
        dve = lambda fn, r, w: S.op("dve", fn, r, w)
        pool = lambda fn, r, w: S.op("pool", fn, r, w)

        free_banks = list(range(8))

        def pb():
            assert free_banks, "PSUM banks exhausted"
            return free_banks.pop(0)

        def pfree(b):
            assert b not in free_banks
            free_banks.append(b)

        fpc = [0]

        def fp():
            i = fpc[0] % NF
            fpc[0] += 1
            return i

        S.op("sp", lambda e: e.dma_start(out=identf[:], in_=cst_d[:, 0:128]), writes=["identf"], dma="c0")
        S.op("sp", lambda e: e.dma_start(out=maskS[:], in_=cst_d[:, 128:256]), writes=["maskS"], dma="c1")
        S.op("sp", lambda e: e.dma_start(out=pv[:], in_=pvec_d[:, :]), writes=["pv"], dma="c2")
        S.op("sp", lambda e: e.dma_start(out=grow[:], in_=grow_d[:, :]), writes=["grow"], dma="c3")
        S.op("sp", lambda e: e.dma_start(out=lnrep[:], in_=lnrep_d[:, :, :]), writes=["lnrep"], dma="c4")
        casts = []
        for c0 in (0, 1540):
            casts.append((sc_in[:, c0:c0 + 1540], w_in_d[:, c0:c0 + 1540], "sc_in"))
        casts.append((sc_out[:, :], w_out_d[:, :], "sc_out"))
        for c0 in (0, 1408):
            casts.append((sc_gate[:, c0:c0 + 1408], w_gate_d[:, c0:c0 + 1408], "sc_gate"))
            casts.append((sc_up[:, c0:c0 + 1408], w_up_d[:, c0:c0 + 1408], "sc_up"))
        for r0 in range(0, DFF, 704):
            casts.append((sc_down[r0:r0 + 704, :], w_down_d[r0:r0 + 704, :], "sc_down"))
        cast_keys = {}
        for i, (o_, i_, nm) in enumerate(casts):
            S.op("pool", lambda e, o_=o_, i_=i_: e.dma_start(out=o_, in_=i_), writes=[(nm, i)], dma=f"cast{i}")
            cast_keys.setdefault(nm, []).append((nm, i))

        dve(lambda e: e.tensor_copy(out=identb[:], in_=identf[:]), ["identf"], ["identb"])
        dve(lambda e: e.memset(mhalf, -0.5), [], ["mhalf"])
        dve(lambda e: e.memset(onescol, 1.0 / 128), [], ["onescol"])
        dve(lambda e: e.memset(ones4[:], 1.0), [], ["ones4"])
        dve(lambda e: e.memset(negbig, -1e30), [], ["negbig"])
        dve(lambda e: e.tensor_scalar(out=mhw5, in0=pv[:, 24:28], scalar1=0.5, scalar2=None, op0=ALU.mult), ["pv"], ["mhw5"])
        dve(lambda e: e.tensor_scalar(out=nfb, in0=grow[:, 1:2], scalar1=-1.0, scalar2=None, op0=ALU.mult), ["grow"], ["nfb"])
        for i in range(32):
            dve(lambda e, i=i: e.tensor_scalar(out=dg4[:, i, :], in0=identb[:], scalar1=pv[:, 40 + i:41 + i], scalar2=None, op0=ALU.mult),
                ["identb", "pv"], ["dg4"] if i in (0, 31) else [])
        S.op("sp", lambda e: e.dma_start(out=wgi[:], in_=sc_in[:, 2048:2052].rearrange("(k p) f -> p k f", p=128)),
             reads=cast_keys["sc_in"], writes=["wg"], dma="c5")
        S.op("sp", lambda e: e.dma_start(out=wgf[:], in_=sc_in[:, 2052:2056].rearrange("(k p) f -> p k f", p=128)),
             reads=cast_keys["sc_in"], writes=["wg"], dma="c6")

        plan = []
        PER = 0
        for g in range(NG):
            pl = []
            for c0 in (0, 512, 1024, 1536, 2056, 2568):
                pl.append((sc_in, 0, 8, c0, 512, "sc_in"))
            for c0 in (0, 512):
                pl.append((sc_out, 0, 8, c0, 512, "sc_out"))
            for p in range(6):
                ncol = 512 if p < 5 else 256
                pl.append((sc_gate, 0, 8, p * 512, ncol, "sc_gate"))
                pl.append((sc_up, 0, 8, p * 512, ncol, "sc_up"))
            for half in range(2):
                for ck in range(3):
                    nk = 8 if ck < 2 else 6
                    pl.append((sc_down, ck * 8, nk, half * 512, 512, "sc_down"))
            PER = len(pl)
            plan += pl

        def wload(i, slot):
            src, k0, nk, c0, ncol, nm = plan[i]
            S.op("sp", lambda e: e.dma_start(out=wring[slot][:, 0:nk, 0:ncol],
                                             in_=src[k0 * 128:(k0 + nk) * 128, c0:c0 + ncol].rearrange("(k p) c -> p k c", p=128)),
                 reads=cast_keys[nm], writes=[("w", slot)], dma=f"w{slot}")

        wr = Ring(R, len(plan), wload)

        def xload(gs, slot):
            g = gs // 4
            j = gs % 4
            s_, m_ = g // nmt, g % nmt
            t0 = m_ * T + j * 128
            S.op("sp", lambda e: e.dma_start(out=xres[slot][:], in_=x_d[s_, t0:t0 + 128, :]), writes=[("x", slot)], dma=f"x{slot}")

        xr = Ring(NX, NG * 4, xload)

        def transposes_to(srcbf, srckey, dst, dstkey, j, scale_cols):
            b = pb()
            for kc in range(8):
                pe(lambda e, kc=kc: e.transpose(out=psbf[b][:, kc * 128:(kc + 1) * 128], in_=srcbf[:, kc * 128:(kc + 1) * 128], identity=identb[:]),
                   [srckey, "identb"], [("ps", b)], sig=(kc == 7))
            dve(lambda e: e.tensor_tensor(out=dst[:, :, j * 128:(j + 1) * 128], in0=psbf[b][:, 0:1024].rearrange("p (k t) -> p k t", k=8),
                                          in1=scale_cols.unsqueeze(2).to_broadcast([128, 8, 128]), op=ALU.mult),
                [("ps", b), "pv"], [(dstkey, j)])
            pfree(b)

        def rstd_from(ss_ap, ms_ap, rs_ap, n, rkeys, key):
            dve(lambda e: e.tensor_scalar(out=ms_ap, in0=ss_ap, scalar1=1.0 / D, scalar2=EPS, op0=ALU.mult, op1=ALU.add), rkeys, [key + "_ms"])
            pool(lambda e: e.tensor_tensor(out=rs_ap, in0=ms_ap, in1=mhalf[:, 0:n], op=ALU.pow), [key + "_ms", "mhalf"], [key])

        XN = [("xnT", j) for j in range(4)]

        def interleave(*gens):
            gens = list(gens)
            while gens:
                for gq in list(gens):
                    try:
                        next(gq)
                    except StopIteration:
                        gens.remove(gq)

        def stageA_front(g):
            for j in range(4):
                gs = 4 * g + j
                sl = xr.need(gs)
                act(lambda e, sl=sl, j=j: e.activation(out=junk[:], in_=xres[sl][:], func=AF.Square, accum_out=ssx[:, j:j + 1]),
                    [("x", sl)], [("ssx", j)])
            rstd_from(ssx, msx, rstdx, 4, [("ssx", j) for j in range(4)], "rstdx")

        def stageA_mid(g):
            for j in range(4):
                gs = 4 * g + j
                sl = gs % NX
                r = j % 2
                act(lambda e, sl=sl, j=j, r=r: e.activation(out=xs[r][:], in_=xres[sl][:], func=AF.Copy, scale=rstdx[:, j:j + 1]),
                    [("x", sl), "rstdx"], [("xs", r)])
                transposes_to(xs[r], ("xs", r), xnT, "xnT", j, pv[:, 0:8])

        def gates_front(g):
            first = (g % nmt == 0)
            bi = pb(); bf_ = pb()
            for (bk, wt) in ((bi, wgi), (bf_, wgf)):
                for kc in range(8):
                    pe(lambda e, bk=bk, wt=wt, kc=kc: e.matmul(ps[bk][0:4, :], lhsT=wt[:, kc, :], rhs=xnT[:, kc, :], start=(kc == 0), stop=(kc == 7)),
                       XN + ["wg"], [("ps", bk)], sig=(kc == 7))
            act(lambda e: e.activation(out=g_e[:], in_=ps[bf_][0:4, :], func=AF.Exp, bias=nfb, scale=-1.0), [("ps", bf_), "nfb"], ["g_e"])
            pfree(bf_)
            act(lambda e: e.activation(out=g_e[:], in_=g_e[:], func=AF.Ln, bias=1.0), ["g_e"], ["g_e"])
            onesb = ones4[:, 0:1].to_broadcast([4, T])
            if first:
                dve(lambda e: e.tensor_tensor_scan(out=g_cs[:], data0=onesb, data1=g_e[:], initial=0.0, op0=ALU.mult, op1=ALU.add),
                    ["g_e", "ones4"], ["g_cs"])
            else:
                dve(lambda e: e.tensor_tensor_scan(out=g_cs[:], data0=onesb, data1=g_e[:], initial=carryB, op0=ALU.mult, op1=ALU.add),
                    ["g_e", "ones4", "carryB"], ["g_cs"])
            dve(lambda e: e.tensor_copy(out=carryB, in_=g_cs[:, T - 1:T]), ["g_cs"], ["carryB"])
            dve(lambda e: e.scalar_tensor_tensor(out=g_gp[:], in0=ps[bi][0:4, :], scalar=grow[:, 0:1], in1=g_cs[:], op0=ALU.add, op1=ALU.add),
                [("ps", bi), "grow", "g_cs"], ["g_gp"])
            pfree(bi)
            dve(lambda e: e.tensor_reduce(out=cm, in_=g_gp[:].rearrange("p (c l) -> p c l", l=128), axis=AX.X, op=ALU.max), ["g_gp"], ["cm"])
            if first:
                dve(lambda e: e.memset(Rt[:, 0:1], 0.0), [], ["Rt"])
            else:
                dve(lambda e: e.tensor_copy(out=Rt[:, 0:1], in_=Rt[:, 4:5]), ["Rt"], ["Rt"])
            dve(lambda e: e.tensor_tensor_scan(out=Rt[:, 1:5], data0=negbig, data1=cm, initial=Rt[:, 0:1], op0=ALU.max, op1=ALU.max),
                ["Rt", "cm", "negbig"], ["Rt"])
            dve(lambda e: e.tensor_scalar(out=negR, in0=Rt[:, 1:5], scalar1=-1.0, scalar2=None, op0=ALU.mult), ["Rt"], ["negR"])
            dve(lambda e: e.tensor_tensor(out=dR, in0=Rt[:, 0:4], in1=Rt[:, 1:5], op=ALU.subtract), ["Rt"], ["dR"])
            act(lambda e: e.activation(out=dec, in_=dR, func=AF.Exp), ["dR"], ["dec"])
            for c in range(4):
                act(lambda e, c=c: e.activation(out=g_gp[:, c * 128:(c + 1) * 128], in_=g_gp[:, c * 128:(c + 1) * 128], func=AF.Exp, bias=negR[:, c:c + 1]),
                    ["g_gp", "negR", "cm"], ["g_gp"])
                act(lambda e, c=c: e.activation(out=g_cs[:, c * 128:(c + 1) * 128], in_=g_cs[:, c * 128:(c + 1) * 128], func=AF.Exp, bias=negR[:, c:c + 1]),
                    ["g_cs", "negR", "carryB"], ["g_cs"])
            dve(lambda e: e.tensor_tensor(out=Dg.rearrange("p (c h) -> p c h", h=4), in0=dec.unsqueeze(2).to_broadcast([4, 4, 4]),
                                          in1=identf[0:4, 0:4].unsqueeze(1).to_broadcast([4, 4, 4]), op=ALU.mult), ["dec", "identf"], ["Dg"])

        def macro(g):
            s_, m_ = g // nmt, g % nmt
            first = (m_ == 0)
            wb = g * PER
            if g == 0:
                stageA_front(0)
                stageA_mid(0)
                gates_front(0)
            def qk_proj(blk):
                wi = wb + (0 if blk < 4 else 1)
                wsl = wr.need(wi)
                b4 = blk % 4
                b = pb()
                for kc in range(8):
                    pe(lambda e, kc=kc: e.matmul(ps[b][:, :], lhsT=wring[wsl][:, kc, b4 * 128:(b4 + 1) * 128], rhs=xnT[:, kc, :],
                                                 start=(kc == 0), stop=(kc == 7)),
                       XN + [("w", wsl)], [("ps", b)], sig=(kc == 7))
                if blk in (3, 7):
                    wr.done(wi)
                r = blk % 2
                stt = qkst[r]
                if first:
                    dve(lambda e: e.memset(stt[:, 0:3], 0.0), [], [("qkst", r)])
                else:
                    dve(lambda e: e.tensor_copy(out=stt[:, 0:3], in_=qhalo[:, blk, :]), [("qhalo", blk)], [("qkst", r)])
                act(lambda e: e.activation(out=stt[:, 3:T + 3], in_=ps[b][:, :], func=AF.Copy), [("ps", b)], [("qkst", r)])
                pfree(b)
                dve(lambda e: e.tensor_copy(out=qhalo[:, blk, :], in_=stt[:, T:T + 3]), [("qkst", r)], [("qhalo", blk)])

            def qk_conv(blk):
                r = blk % 2
                stt = qkst[r]
                b2 = pb()
                for k in range(4):
                    pe(lambda e, k=k: e.matmul(ps[b2][:, :], lhsT=dg4[:, k * 8 + blk, :], rhs=stt[:, k:k + T], start=(k == 0), stop=(k == 3)),
                       [("qkst", r), "dg4"], [("ps", b2)], sig=(k == 3))
                act(lambda e: e.activation(out=qkT[:, blk, :], in_=ps[b2][:, :], func=AF.Silu, bias=pv[:, 16 + blk:17 + blk]),
                    [("ps", b2), "pv"], [("qkT", blk)])
                pfree(b2)

            for blk in range(8):
                qk_proj(blk)
                if blk > 0:
                    qk_conv(blk - 1)
            qk_conv(7)
            wi_v = wb + 2
            wsl_v = wr.need(wi_v)
            wi_o = wb + 3
            wsl_o = wr.need(wi_o)

            def tokmajor_proj(wsl, j):
                b = pb()
                for kc in range(8):
                    pe(lambda e, kc=kc: e.matmul(ps[b][:, :], lhsT=xnT[:, kc, j * 128:(j + 1) * 128], rhs=wring[wsl][:, kc, :],
                                                 start=(kc == 0), stop=(kc == 7)),
                       [("xnT", j), ("w", wsl)], [("ps", b)], sig=(kc == 7))
                return b

            bt = pb()
            for c in range(4):
                pe(lambda e, c=c: e.transpose(out=ps[bt][:, c * 8:c * 8 + 4], in_=g_gp[0:4, c * 128:(c + 1) * 128], identity=identf[0:4, 0:4]),
                   ["g_gp", "identf"], [("ps", bt)], sig=False)
                pe(lambda e, c=c: e.transpose(out=ps[bt][:, c * 8 + 4:c * 8 + 8], in_=g_cs[0:4, c * 128:(c + 1) * 128], identity=identf[0:4, 0:4]),
                   ["g_cs", "identf"], [("ps", bt)], sig=(c == 3))
            dve(lambda e: e.tensor_copy(out=gtok, in_=ps[bt][:, 0:32]), [("ps", bt)], ["gtok"])
            pfree(bt)
            bd = pb()
            pe(lambda e: e.matmul(ps[bd][:, 0:16], lhsT=ones4[0:4, :], rhs=Dg, start=True, stop=True), ["ones4", "Dg"], [("ps", bd)])
            dve(lambda e: e.tensor_copy(out=decb, in_=ps[bd][:, 0:16]), [("ps", bd)], ["decb"])
            pfree(bd)
            dve(lambda e: e.tensor_scalar(out=decbq, in0=decb, scalar1=QSCALE, scalar2=None, op0=ALU.mult), ["decb"], ["decbq"])
            for j in range(4):
                b = tokmajor_proj(wsl_v, j)
                dve(lambda e, b=b, j=j: e.tensor_tensor(out=vp[:, j, :, 0:128], in0=ps[b][:, :].rearrange("p (h e) -> p h e", h=4),
                                                        in1=gtok[:, j * 8:j * 8 + 4].unsqueeze(2).to_broadcast([128, 4, 128]), op=ALU.mult),
                    [("ps", b), "gtok"], [("vp", j)])
                dve(lambda e, j=j: e.tensor_copy(out=vp[:, j, :, 128:129], in_=gtok[:, j * 8:j * 8 + 4].unsqueeze(2)), ["gtok"], [("vp", j)])
                pfree(b)
            wr.done(wi_v)
            for j in range(4):
                b = tokmajor_proj(wsl_o, j)
                act(lambda e, b=b, j=j: e.activation(out=tho[:, j, :], in_=ps[b][:, :], func=AF.Tanh, scale=0.5), [("ps", b)], [("tho", j)])
                pfree(b)
            wr.done(wi_o)

            wcv = wr.need(wb + 4)
            wcg = wr.need(wb + 5)

            def conv_gen():
                st = {}

                def c1(b4):
                    r = b4 % 2
                    dgt = dg31[r]
                    pool(lambda e: e.tensor_tensor(out=dgt[:, :, :], in0=identb[:].unsqueeze(1).to_broadcast([128, 31, 128]),
                                                   in1=pv[:, 72 + b4 * 31:72 + (b4 + 1) * 31].unsqueeze(2).to_broadcast([128, 31, 128]), op=ALU.mult),
                         ["identb", "pv"], [("dg31", r)])
                    bg = pb()
                    for kc in range(8):
                        pe(lambda e, kc=kc: e.matmul(ps[bg][:, :], lhsT=wring[wcg][:, kc, b4 * 128:(b4 + 1) * 128], rhs=xnT[:, kc, :],
                                                     start=(kc == 0), stop=(kc == 7)), XN + [("w", wcg)], [("ps", bg)], sig=(kc == 7))
                    bv = pb()
                    for kc in range(8):
                        pe(lambda e, kc=kc: e.matmul(ps[bv][:, :], lhsT=wring[wcv][:, kc, b4 * 128:(b4 + 1) * 128], rhs=xnT[:, kc, :],
                                                     start=(kc == 0), stop=(kc == 7)), XN + [("w", wcv)], [("ps", bv)], sig=(kc == 7))
                    if b4 == 3:
                        wr.done(wb + 4)
                        wr.done(wb + 5)
                    f1 = fp()
                    act(lambda e: e.activation(out=fpool[f1][:], in_=ps[bg][:, :], func=AF.Tanh, scale=0.5), [("ps", bg)], [("f", f1)])
                    pfree(bg)
                    ut = ust[r]
                    if first:
                        dve(lambda e: e.memset(ut[:, 0:30], 0.0), [], [("ust", r)])
                    else:
                        dve(lambda e: e.tensor_copy(out=ut[:, 0:30], in_=uhalo[:, b4, :]), [("uhalo", b4)], [("ust", r)])
                    dve(lambda e: e.scalar_tensor_tensor(out=ut[:, 30:T + 30], in0=fpool[f1][:], scalar=1.0, in1=ps[bv][:, :],
                                                         op0=ALU.add, op1=ALU.mult),
                        [("f", f1), ("ps", bv)], [("ust", r)])
                    pfree(bv)
                    dve(lambda e: e.tensor_copy(out=uhalo[:, b4, :], in_=ut[:, T:T + 30]), [("ust", r)], [("uhalo", b4)])

                def c2(b4):
                    r = b4 % 2
                    dgt = dg31[r]
                    ut = ust[r]
                    bc = pb()
                    for k in range(31):
                        pe(lambda e, k=k: e.matmul(ps[bc][:, :], lhsT=dgt[:, k, :], rhs=ut[:, k:k + T], start=(k == 0), stop=(k == 30)),
                           [("ust", r), ("dg31", r)], [("ps", bc)], sig=(k == 30))
                    fu = fp()
                    fq = fp()
                    st[b4] = (fu, fq)
                    act(lambda e: e.activation(out=fpool[fu][:], in_=ps[bc][:, :], func=AF.Identity, bias=pv[:, 28 + b4:29 + b4], scale=0.5),
                        [("ps", bc), "pv"], [("f", fu)])
                    pfree(bc)
                    act(lambda e: e.activation(out=fpool[fq][:], in_=fpool[fu][:], func=AF.Square), [("f", fu)], [("f", fq)])

                def c3(b4):
                    fu, fq = st[b4]
                    bs = pb()
                    for j in range(4):
                        pe(lambda e, j=j: e.matmul(ps[bs][:, 2 * j:2 * j + 1], lhsT=fpool[fu][:, j * 128:(j + 1) * 128], rhs=onescol, start=True, stop=True),
                           [("f", fu), "onescol"], [("ps", bs)], sig=False)
                        pe(lambda e, j=j: e.matmul(ps[bs][:, 2 * j + 1:2 * j + 2], lhsT=fpool[fq][:, j * 128:(j + 1) * 128], rhs=onescol, start=True, stop=True),
                           [("f", fq), "onescol"], [("ps", bs)], sig=(j == 3))
                    dve(lambda e: e.tensor_copy(out=cst8, in_=ps[bs][:, 0:8]), [("ps", bs)], ["cst8"])
                    pfree(bs)
                    cv_ = cst8.rearrange("p (j q) -> p j q", q=2)
                    dve(lambda e: e.tensor_tensor(out=c4a, in0=cv_[:, :, 0], in1=cv_[:, :, 0], op=ALU.mult), ["cst8"], ["c4a"])
                    dve(lambda e: e.scalar_tensor_tensor(out=c4b, in0=c4a, scalar=-1.0, in1=cv_[:, :, 1], op0=ALU.mult, op1=ALU.add), ["c4a", "cst8"], ["c4b"])
                    dve(lambda e: e.tensor_scalar(out=c4b, in0=c4b, scalar1=EPS, scalar2=None, op0=ALU.add), ["c4b"], ["c4b"])
                    pool(lambda e: e.tensor_tensor(out=c4c, in0=c4b, in1=mhalf[:, 0:4], op=ALU.pow), ["c4b", "mhalf"], ["c4c"])
                    dve(lambda e: e.scalar_tensor_tensor(out=c4d, in0=cv_[:, :, 0], scalar=-1.0, in1=c4c, op0=ALU.mult, op1=ALU.mult), ["cst8", "c4c"], ["c4d"])
                    dve(lambda e: e.tensor_copy(out=rbm[:, 0, :, :], in_=c4c.unsqueeze(2).to_broadcast([128, 4, 128])), ["c4c"], ["rbm0"])
                    dve(lambda e: e.tensor_copy(out=rbm[:, 1, :, :], in_=c4d.unsqueeze(2).to_broadcast([128, 4, 128])), ["c4d"], ["rbm1"])

                def c4(b4):
                    fu, fq = st[b4]
                    br = pb(); bn = pb()
                    for j in range(4):
                        pe(lambda e, j=j: e.transpose(out=ps[br][:, j * 128:(j + 1) * 128], in_=rbm[:, 0, j, :], identity=identf[:]),
                           ["rbm0", "identf"], [("ps", br)], sig=(j == 3))
                    for j in range(4):
                        pe(lambda e, j=j: e.transpose(out=ps[bn][:, j * 128:(j + 1) * 128], in_=rbm[:, 1, j, :], identity=identf[:]),
                           ["rbm1", "identf"], [("ps", bn)], sig=(j == 3))
                    dve(lambda e: e.tensor_tensor(out=fpool[fu][:], in0=fpool[fu][:], in1=ps[br][:, :], op=ALU.mult), [("f", fu), ("ps", br)], [("f", fu)])
                    dve(lambda e: e.tensor_tensor(out=fpool[fu][:], in0=fpool[fu][:], in1=ps[bn][:, :], op=ALU.add), [("f", fu), ("ps", bn)], [("f", fu)])
                    pfree(br)
                    pfree(bn)
                    act(lambda e: e.activation(out=yT[:, 4 + b4, :], in_=fpool[fu][:], func=AF.Silu, bias=pv[:, 36 + b4:37 + b4], scale=pv[:, 32 + b4:33 + b4]),
                        [("f", fu), "pv"], [("yTc", b4)])

                order = [[(c1, 0)], [(c1, 1), (c2, 0)], [(c3, 0), (c2, 1)], [(c4, 0), (c1, 2)], [(c3, 1), (c2, 2)],
                         [(c4, 1), (c1, 3)], [(c3, 2), (c2, 3)], [(c4, 2)], [(c3, 3)], [(c4, 3)]]
                for grp in order:
                    for fn_, b4 in grp:
                        fn_(b4)
                    yield

            def mlstm_gen():
                if first:
                    dve(lambda e: e.memset(Cf[:], 0.0), [], [("Cf", h) for h in range(4)])
                st = {}

                def pa(j):
                    tk = slice(j * 128, (j + 1) * 128)
                    for h in range(4):
                        ci = j * 4 + h
                        act(lambda e, h=h, ci=ci: e.activation(out=Cdb[:, h, :], in_=Cf[:, h, :], func=AF.Copy, scale=decbq[:, ci:ci + 1]),
                            [("Cf", h), "decbq"], [("Cdb", h)])
                    bS = pb(); bK = pb()
                    st[j] = (bS, bK)
                    for h in range(4):
                        pe(lambda e, h=h: e.matmul(ps[bS][:, h * 128:(h + 1) * 128], lhsT=qkT[:, 4 + h, tk], rhs=qkT[:, h, tk], start=True, stop=True),
                           [("qkT", 4 + h), ("qkT", h)], [("ps", bS)], sig=(h == 3))
                    for h in range(4):
                        pe(lambda e, h=h: e.transpose(out=psbf[bK][:, h * 128:(h + 1) * 128], in_=qkT[:, 4 + h, tk], identity=identb[:]),
                           [("qkT", 4 + h), "identb"], [("ps", bK)], sig=(h == 3))

                def pbb(j):
                    tk = slice(j * 128, (j + 1) * 128)
                    bS, bK = st[j]
                    for h in range(4):
                        dve(lambda e, h=h: e.tensor_tensor(out=stb[h][:], in0=ps[bS][:, h * 128:(h + 1) * 128], in1=maskS[:], op=ALU.mult),
                            [("ps", bS), "maskS"], [("stb", h)])
                        act(lambda e, h=h: e.activation(out=ktok[h][:], in_=psbf[bK][:, h * 128:(h + 1) * 128], func=AF.Copy), [("ps", bK)], [("ktok", h)])
                    pfree(bS)
                    pfree(bK)
                    bN = [pb(), pb()]
                    bV = [pb(), pb()]
                    for h in range(4):
                        o_ = ps[bN[h // 2]][:, (h % 2) * 129:(h % 2) * 129 + 129]
                        pe(lambda e, h=h, o_=o_: e.matmul(o_, lhsT=qkT[:, h, tk], rhs=Cdb[:, h, :], start=True, stop=False),
                           [("qkT", h), ("Cdb", h)], [("ps", bN[h // 2])], sig=False)
                        pe(lambda e, h=h, o_=o_: e.matmul(o_, lhsT=stb[h][:], rhs=vp[:, j, h, :], start=False, stop=True),
                           [("stb", h), ("vp", j)], [("ps", bN[h // 2])], sig=(h % 2 == 1))
                    for h in range(4):
                        o_ = ps[bV[h // 2]][:, (h % 2) * 129:(h % 2) * 129 + 129]
                        pe(lambda e, h=h, o_=o_: e.matmul(o_, lhsT=ktok[h][:], rhs=vp[:, j, h, :], start=True, stop=True),
                           [("ktok", h), ("vp", j)], [("ps", bV[h // 2])], sig=(h % 2 == 1))
                    for h in range(4):
                        o_ = ps[bV[h // 2]][:, (h % 2) * 129:(h % 2) * 129 + 129]
                        ci = j * 4 + h
                        dve(lambda e, h=h, o_=o_, ci=ci: e.scalar_tensor_tensor(out=Cf[:, h, :], in0=Cf[:, h, :], scalar=decb[:, ci:ci + 1], in1=o_,
                                                                            op0=ALU.mult, op1=ALU.add),
                            [("ps", bV[h // 2]), ("Cf", h), "decb"], [("Cf", h)])
                    pfree(bV[0])
                    pfree(bV[1])
                    for i in range(2):
                        dcol = ps[bN[i]][:, 0:258].rearrange("p (h e) -> p h e", e=129)[:, :, 128]
                        act(lambda e, i=i, dcol=dcol: e.activation(out=absd[:, 2 * i:2 * i + 2], in_=dcol, func=AF.Abs),
                            [("ps", bN[i])], [("absd", i)])
                    dve(lambda e: e.tensor_tensor(out=ddv, in0=absd, in1=gtok[:, j * 8 + 4:j * 8 + 8], op=ALU.max),
                        [("absd", 0), ("absd", 1), "gtok"], ["ddv"])
                    dve(lambda e: e.reciprocal(out=rec, in_=ddv), ["ddv"], ["rec"])
                    for h in range(4):
                        nb_ = ps[bN[h // 2]]
                        c0 = (h % 2) * 129
                        act(lambda e, h=h, nb_=nb_, c0=c0: e.activation(out=hn[:, h, :], in_=nb_[:, c0:c0 + 128], func=AF.Copy, scale=rec[:, h:h + 1]),
                            [("ps", bN[h // 2]), "rec"], [("hn", h)])
                        dve(lambda e, h=h: e.bn_stats(out=bst[:, h * 6:(h + 1) * 6], in_=hn[:, h, :]), [("hn", h)], [("bst", h)])
                        dve(lambda e, h=h: e.bn_aggr(out=mv[:, h * 2:(h + 1) * 2], in_=bst[:, h * 6:(h + 1) * 6]), [("bst", h)], [("mv", h)])
                    pfree(bN[0])
                    pfree(bN[1])
                    mvv = mv.rearrange("p (h q) -> p h q", q=2)
                    MV = [("mv", h) for h in range(4)]
                    dve(lambda e: e.tensor_scalar(out=e4a, in0=mvv[:, :, 1], scalar1=EPS, scalar2=None, op0=ALU.add), MV, ["e4a"])
                    pool(lambda e: e.tensor_tensor(out=e4b, in0=e4a, in1=mhalf[:, 0:4], op=ALU.pow), ["e4a", "mhalf"], ["e4b"])
                    dve(lambda e: e.scalar_tensor_tensor(out=e4a, in0=mvv[:, :, 0], scalar=-1.0, in1=e4b, op0=ALU.mult, op1=ALU.mult), MV + ["e4b"], ["e4a"])
                    for h in range(4):
                        act(lambda e, h=h: e.activation(out=hn[:, h, :], in_=hn[:, h, :], func=AF.Identity, bias=e4a[:, h:h + 1], scale=e4b[:, h:h + 1]),
                            [("hn", h), "e4a", "e4b"], [("hn", h)])
                    r = j % 2
                    dve(lambda e: e.scalar_tensor_tensor(out=ytok[r][:], in0=tho[:, j, :], scalar=1.0, in1=hn[:].rearrange("p h e -> p (h e)"),
                                                         op0=ALU.add, op1=ALU.mult),
                        [("tho", j)] + [("hn", h) for h in range(4)], [("ytok", r)])

                def pc(j):
                    tk = slice(j * 128, (j + 1) * 128)
                    r = j % 2
                    b5 = pb()
                    for h in range(4):
                        pe(lambda e, h=h: e.transpose(out=psbf[b5][:, h * 128:(h + 1) * 128], in_=ytok[r][:, h * 128:(h + 1) * 128], identity=identb[:]),
                           [("ytok", r), "identb"], [("ps", b5)], sig=(h == 3))
                    dve(lambda e: e.tensor_tensor(out=yT[:, 0:4, tk], in0=psbf[b5][:, 0:512].rearrange("p (h t) -> p h t", h=4),
                                                  in1=mhw5.unsqueeze(2).to_broadcast([128, 4, 128]), op=ALU.mult),
                        [("ps", b5), "mhw5"], [("yTm", j)])
                    pfree(b5)

                order = [[(pa, 0)], [(pbb, 0)], [(pa, 1)], [(pbb, 1)], [(pc, 0), (pa, 2)], [(pbb, 2)], [(pc, 1), (pa, 3)], [(pbb, 3)], [(pc, 2)], [(pc, 3)]]
                for grp in order:
                    for fn_, j in grp:
                        fn_(j)
                    yield

            cg_ = conv_gen()
            next(cg_)
            next(cg_)
            mg_ = mlstm_gen()

            wo0 = wr.need(wb + 6)
            wo1 = wr.need(wb + 7)
            YC = [("yTc", b) for b in range(4)]

            def epi_ss(bk, col, key):
                act(lambda e: e.activation(out=junk[:, 0:512], in_=ps[bk][:, :], func=AF.Square, accum_out=sm[:, col:col + 1]), [("ps", bk)], [key])

            def epi_fin(bk, lni, half, sl):
                hs = slice(half * 512, (half + 1) * 512)
                dve(lambda e: e.tensor_tensor(out=ps[bk][:, :], in0=ps[bk][:, :], in1=lnrep[:, lni, hs], op=ALU.mult), [("ps", bk), "lnrep"], [("ps", bk)])
                dve(lambda e: e.scalar_tensor_tensor(out=xres[sl][:, hs], in0=ps[bk][:, :], scalar=sm[:, 161:162], in1=xres[sl][:, hs], op0=ALU.mult, op1=ALU.add),
                    [("ps", bk), "ep_rs", ("x", sl)], [("x", sl)])
                pfree(bk)

            def epilogue(j, ba, bb, lni, sl):
                epi_ss(ba, 157, "ep_s0")
                epi_ss(bb, 158, "ep_s1")
                dve(lambda e: e.tensor_tensor(out=sm[:, 159:160], in0=sm[:, 157:158], in1=sm[:, 158:159], op=ALU.add), ["ep_s0", "ep_s1"], ["ep_ss"])
                rstd_from(sm[:, 159:160], sm[:, 160:161], sm[:, 161:162], 1, ["ep_ss"], "ep_rs")
                epi_fin(ba, lni, 0, sl)
                epi_fin(bb, lni, 1, sl)

            dstage = {}

            def epilogue_d_early(j, b0):
                act(lambda e: e.activation(out=junk[:, 0:512], in_=ps[b0][:, :], func=AF.Square, accum_out=sm[:, 181 + j:182 + j]), [("ps", b0)], [("ds0", j)])
                f = fp()
                dstage[j] = f
                dve(lambda e: e.tensor_tensor(out=fpool[f][:], in0=ps[b0][:, :], in1=lnrep[:, 1, 0:512], op=ALU.mult), [("ps", b0), "lnrep"], [("f", f)])
                pfree(b0)

            def epilogue_d_late(j, b1, sl):
                f = dstage[j]
                epi_ss(b1, 158, "ep_s1")
                dve(lambda e: e.tensor_tensor(out=sm[:, 159:160], in0=sm[:, 181 + j:182 + j], in1=sm[:, 158:159], op=ALU.add), [("ds0", j), "ep_s1"], ["ep_ss"])
                rstd_from(sm[:, 159:160], sm[:, 160:161], sm[:, 161:162], 1, ["ep_ss"], "ep_rs")
                dve(lambda e: e.scalar_tensor_tensor(out=xres[sl][:, 0:512], in0=fpool[f][:], scalar=sm[:, 161:162], in1=xres[sl][:, 0:512], op0=ALU.mult, op1=ALU.add),
                    [("f", f), "ep_rs", ("x", sl)], [("x", sl)])
                epi_fin(b1, 1, 1, sl)

            def wstage(j):
                gs = 4 * g + j
                sl = gs % NX
                tk = slice(j * 128, (j + 1) * 128)
                ba = pb(); bb = pb()
                for (bk, wsl) in ((ba, wo0), (bb, wo1)):
                    for kc in range(8):
                        pe(lambda e, bk=bk, wsl=wsl, kc=kc: e.matmul(ps[bk][:, :], lhsT=yT[:, kc, tk], rhs=wring[wsl][:, kc, :], start=(kc == 0), stop=(kc == 7)),
                           [("yTm", j), ("w", wsl)] + YC, [("ps", bk)], sig=(kc == 7))
                if j == 3:
                    wr.done(wb + 6)
                    wr.done(wb + 7)
                epilogue(j, ba, bb, 0, sl)
                act(lambda e: e.activation(out=junk[:], in_=xres[sl][:], func=AF.Square, accum_out=sm[:, 164:165]), [("x", sl)], ["hss"])
                rstd_from(sm[:, 164:165], sm[:, 162:163], sm[:, 163:164], 1, ["hss"], "hrs")
                r = j % 2
                act(lambda e: e.activation(out=xs[r][:], in_=xres[sl][:], func=AF.Copy, scale=sm[:, 163:164]), [("x", sl), "hrs"], [("xs", r)])

            def tstage(j):
                r = j % 2
                transposes_to(xs[r], ("xs", r), hnT, "xnT", j, pv[:, 8:16])

            M = lambda: next(mg_, None)
            C = lambda: next(cg_, None)
            M(); C(); C(); M(); C(); M(); C(); C(); M(); C(); M(); C(); C(); M()
            assert next(cg_, "end") == "end"
            wstage(0); M(); wstage(1); tstage(0); M(); M(); wstage(2); tstage(1); M()
            assert next(mg_, "end") == "end"
            wstage(3); tstage(2); tstage(3)

            for p in range(6):
                wg_ = wr.need(wb + 8 + 2 * p)
                wu_ = wr.need(wb + 9 + 2 * p)
                nb = 4 if p < 5 else 2
                for bq in range(nb):
                    fb = p * 4 + bq
                    bg = pb()
                    for kc in range(8):
                        pe(lambda e, bg=bg, kc=kc, bq=bq, wg_=wg_: e.matmul(ps[bg][:, :], lhsT=wring[wg_][:, kc, bq * 128:(bq + 1) * 128], rhs=hnT[:, kc, :],
                                                                           start=(kc == 0), stop=(kc == 7)), XN + [("w", wg_)], [("ps", bg)], sig=(kc == 7))
                    bu = pb()
                    for kc in range(8):
                        pe(lambda e, bu=bu, kc=kc, bq=bq, wu_=wu_: e.matmul(ps[bu][:, :], lhsT=wring[wu_][:, kc, bq * 128:(bq + 1) * 128], rhs=hnT[:, kc, :],
                                                                           start=(kc == 0), stop=(kc == 7)), XN + [("w", wu_)], [("ps", bu)], sig=(kc == 7))
                    f1 = fp()
                    act(lambda e, f1=f1, bg=bg: e.activation(out=fpool[f1][:], in_=ps[bg][:, :], func=AF.Silu), [("ps", bg)], [("f", f1)])
                    pfree(bg)
                    dve(lambda e, f1=f1, bu=bu, fb=fb: e.tensor_tensor(out=actT[:, fb, :], in0=fpool[f1][:], in1=ps[bu][:, :], op=ALU.mult),
                        [("f", f1), ("ps", bu)], [("actT", fb)])
                    pfree(bu)
                wr.done(wb + 8 + 2 * p)
                wr.done(wb + 9 + 2 * p)

            AT = [("actT", fb) for fb in range(NFB)]
            banks = {}
            nxt = g + 1 if g + 1 < NG else None
            if nxt is not None:
                stageA_front(nxt)
            for half in range(2):
                for j in range(4):
                    banks[(j, half)] = pb()
                for ck in range(3):
                    wi = wb + 20 + half * 3 + ck
                    wsl = wr.need(wi)
                    nk = 8 if ck < 2 else 6
                    for j in range(4):
                        tk = slice(j * 128, (j + 1) * 128)
                        bk = banks[(j, half)]
                        for q in range(nk):
                            fc = ck * 8 + q
                            pe(lambda e, bk=bk, fc=fc, q=q, wsl=wsl, tk=tk: e.matmul(ps[bk][:, :], lhsT=actT[:, fc, tk], rhs=wring[wsl][:, q, :],
                                                                                 start=(fc == 0), stop=(fc == NFB - 1)),
                               [("actT", fc), ("w", wsl)], [("ps", bk)], sig=(q == nk - 1))
                    wr.done(wi)
                if half == 0:
                    for j in range(4):
                        epilogue_d_early(j, banks[(j, 0)])
                    if nxt is not None:
                        stageA_mid(nxt)
                if half == 1:
                    for j in range(4):
                        gs = 4 * g + j
                        sl = gs % NX
                        epilogue_d_late(j, banks[(j, 1)], sl)
                        t0 = m_ * T + j * 128
                        S.op("pool", lambda e, sl=sl, t0=t0: e.dma_start(out=out_d[s_, t0:t0 + 128, :], in_=xres[sl][:]), reads=[("x", sl)], dma=f"st{sl}")
                        xr.done(gs)
            if nxt is not None:
                gates_front(nxt)

        for g in range(NG):
            macro(g)
        S.final_wait("pool")
        S.emit()
    return nc


def host_prep(inputs):
    f = np.float32
    pvec = np.zeros((128, 196), f)
    pvec[:, 0:8] = np.asarray(inputs["ln_mix_pre"], f)[0].reshape(8, 128).T
    pvec[:, 8:16] = np.asarray(inputs["ln_ffn_pre"], f)[0].reshape(8, 128).T
    pvec[:, 16:24] = np.asarray(inputs["qk_conv_b"], f)[0].reshape(8, 128).T
    pvec[:, 24:28] = np.asarray(inputs["mh_norm_w"], f)[0].reshape(4, 128).T
    pvec[:, 28:32] = np.asarray(inputs["dw_conv_b"], f)[0].reshape(4, 128).T
    pvec[:, 32:36] = np.asarray(inputs["conv_norm_w"], f)[0].reshape(4, 128).T
    pvec[:, 36:40] = np.asarray(inputs["conv_norm_b"], f)[0].reshape(4, 128).T
    pvec[:, 40:72] = np.asarray(inputs["qk_conv_w"], f)[0].reshape(4, 8, 128).transpose(2, 0, 1).reshape(128, 32)
    pvec[:, 72:196] = np.asarray(inputs["dw_conv_w"], f)[0].reshape(31, 4, 128).transpose(2, 1, 0).reshape(128, 124)
    grow = np.stack([np.asarray(inputs["i_bias"], f)[0], np.asarray(inputs["f_bias"], f)[0]], axis=1).astype(f)
    lnrep = np.ascontiguousarray(np.broadcast_to(
        np.stack([np.asarray(inputs["ln_mix_post"], f)[0], np.asarray(inputs["ln_ffn_post"], f)[0]], axis=0)[None], (128, 2, D))).astype(f)
    cst = np.zeros((128, 256), f)
    cst[:, 0:128] = np.eye(128, dtype=f)
    si = np.arange(128)
    cst[:, 128:256] = np.where(si[:, None] <= si[None, :], f(QSCALE), f(0.0))
    return dict(pvec=pvec, grow=grow, lnrep=lnrep, cst=cst)


def run(inputs, nseq, nmt, ncores):
    x = np.asarray(inputs["x"], np.float32)
    common = host_prep(inputs)
    for k in ("w_in", "w_out", "w_gate", "w_up", "w_down"):
        common[k] = np.ascontiguousarray(np.asarray(inputs[k], np.float32))
    nc = build(nseq, nmt)
    in_maps = []
    for c in range(ncores):
        m = dict(common)
        m["x"] = np.ascontiguousarray(x[c * nseq:(c + 1) * nseq])
        in_maps.append(m)
    res = run_bass_kernel_spmd(nc, in_maps, core_ids=list(range(ncores)), **RUN_KW)
    return np.concatenate([np.asarray(r["out"]) for r in res.results], axis=0).astype(np.float32)


def kernel(**inputs):
    return run(inputs, 2, 8, NCORES)
```

```python
import numpy as np
from contextlib import ExitStack
import concourse.bass as bass
import concourse.mybir as mybir
from concourse.bass_utils import run_bass_kernel_spmd

F32 = mybir.dt.float32
BF16 = mybir.dt.bfloat16
AF = mybir.ActivationFunctionType
ALU = mybir.AluOpType
AX = mybir.AxisListType

D = 1024
DIN = 3080
DFF = 2816
NFB = 22
EPS = 1e-6
QSCALE = 128 ** -0.5
NCORES = 8
RUN_KW = {}


class Sched:
    CE = ("pe", "act", "dve", "pool")
    ENG = ("pe", "act", "dve", "pool", "sp")

    def __init__(self, nc, es):
        self.nc = nc
        self.es = es
        self.q = {e: [] for e in self.ENG}
        self.sem = {e: es.enter_context(nc.semaphore(f"s_{e}")) for e in self.CE}
        self.cnt = {e: 0 for e in self.CE}
        self.dsem = {}
        self.waited = {e: {} for e in self.ENG}
        self.lastw = {}
        self.readers = {}

    def _semobj(self, key):
        if key in self.sem:
            return self.sem[key]
        return self.dsem[key][0]

    def _need(self, eng, tok, waits):
        if tok is None:
            return
        key, val = tok
        if key == "pe" and eng == "pe":
            return
        if self.waited[eng].get(key, 0) >= val:
            return
        if waits.get(key, 0) < val:
            waits[key] = val

    def op(self, eng, fn, reads=(), writes=(), signal=True, dma=None):
        waits = {}
        for r in reads:
            self._need(eng, self.lastw.get(r), waits)
            if isinstance(r, tuple) and r[0] == "ps":
                for t in self.readers.get(r, {}).items():
                    if t[0] != eng:
                        self._need(eng, t, waits)
        for w in writes:
            self._need(eng, self.lastw.get(w), waits)
            for t in self.readers.get(w, {}).items():
                self._need(eng, t, waits)
        if dma is not None:
            if dma not in self.dsem:
                self.dsem[dma] = [self.es.enter_context(self.nc.semaphore(f"d_{len(self.dsem)}")), 0]
            ent = self.dsem[dma]
            ent[1] += 16
            tok = (dma, ent[1])
            sig = (ent[0], 16)
        elif signal:
            self.cnt[eng] += 1
            tok = (eng, self.cnt[eng])
            sig = (self.sem[eng], 1)
        else:
            tok = (eng, self.cnt[eng] + 1)
            sig = None
        wl = [(self._semobj(k), v) for k, v in waits.items()]
        for k, v in waits.items():
            self.waited[eng][k] = v
        self.q[eng].append((fn, wl, sig))
        for r in reads:
            d = self.readers.setdefault(r, {})
            if d.get(tok[0], 0) < tok[1]:
                d[tok[0]] = tok[1]
        for w in writes:
            self.lastw[w] = tok
            self.readers[w] = {}
        return tok

    def final_wait(self, eng):
        wl = [(ent[0], ent[1]) for k, ent in self.dsem.items()]
        self.q[eng].append((None, wl, None))

    def emit(self):
        nc = self.nc

        def run(e, lst):
            for fn, wl, sig in lst:
                for s, v in wl:
                    e.wait_ge(s, v)
                if fn is None:
                    continue
                ins = fn(e)
                if sig is not None:
                    ins.then_inc(sig[0], sig[1])

        with nc.Block() as block:
            @block.tensor
            def _(e):
                run(e, self.q["pe"])

            @block.scalar
            def _(e):
                run(e, self.q["act"])

            @block.vector
            def _(e):
                run(e, self.q["dve"])

            @block.gpsimd
            def _(e):
                run(e, self.q["pool"])

            @block.sync
            def _(e):
                run(e, self.q["sp"])


class Ring:
    def __init__(self, nslots, nitems, loadfn):
        self.n = nslots
        self.nitems = nitems
        self.loadfn = loadfn
        self.next_load = 0
        self.done_upto = 0

    def pump(self):
        while self.next_load < self.nitems and self.next_load <= self.done_upto + self.n - 1:
            self.loadfn(self.next_load, self.next_load % self.n)
            self.next_load += 1

    def need(self, i):
        self.pump()
        assert i < self.next_load, (i, self.next_load, self.done_upto)
        return i % self.n

    def done(self, i):
        assert i == self.done_upto, (i, self.done_upto)
        self.done_upto = i + 1
        self.pump()


def build(nseq, nmt):
    T = 512
    seq = nmt * T
    NG = nseq * nmt
    nc = bass.Bass("TRN2", target_bir_lowering=False)
    dt_in = lambda name, shape: nc.dram_tensor(name, shape, F32, kind="ExternalInput").ap()
    x_d = dt_in("x", [nseq, seq, D])
    w_in_d = dt_in("w_in", [1, D, DIN])[0]
    w_out_d = dt_in("w_out", [1, D, D])[0]
    w_gate_d = dt_in("w_gate", [1, D, DFF])[0]
    w_up_d = dt_in("w_up", [1, D, DFF])[0]
    w_down_d = dt_in("w_down", [1, DFF, D])[0]
    pvec_d = dt_in("pvec", [128, 196])
    grow_d = dt_in("grow", [4, 2])
    lnrep_d = dt_in("lnrep", [128, 2, D])
    cst_d = dt_in("cst", [128, 256])
    out_d = nc.dram_tensor("out", [nseq, seq, D], F32, kind="ExternalOutput").ap()
    sc_in = nc.dram_tensor("sc_in", [D, DIN], BF16).ap()
    sc_out = nc.dram_tensor("sc_out", [D, D], BF16).ap()
    sc_gate = nc.dram_tensor("sc_gate", [D, DFF], BF16).ap()
    sc_up = nc.dram_tensor("sc_up", [D, DFF], BF16).ap()
    sc_down = nc.dram_tensor("sc_down", [DFF, D], BF16).ap()

    with ExitStack() as es:
        S = Sched(nc, es)
        TL = lambda name, shape, dt: es.enter_context(nc.sbuf_tensor("sb_" + name, shape, dt))
        R = 5
        NX = 8
        wring = [TL(f"wring{i}", [128, 8, 512], BF16) for i in range(R)]
        xres = [TL(f"xres{i}", [128, D], F32) for i in range(NX)]
        xs = [TL(f"xs{i}", [128, D], BF16) for i in range(2)]
        junk = TL("junk", [128, D], BF16)
        xnT = TL("xnT", [128, 8, T], BF16)
        hnT = xnT
        qkst = [TL(f"qkst{i}", [128, T + 3], BF16) for i in range(2)]
        qhalo = TL("qhalo", [128, 8, 3], BF16)
        qkT = TL("qkT", [128, 8, T], BF16)
        vp = TL("vp", [128, 4, 4, 129], BF16)
        tho = TL("tho", [128, 4, T], BF16)
        ust = [TL(f"ust{i}", [128, T + 30], BF16) for i in range(2)]
        uhalo = TL("uhalo", [128, 4, 30], BF16)
        dg31 = [TL(f"dg31_{i}", [128, 31, 128], BF16) for i in range(2)]
        dg4 = TL("dg4", [128, 32, 128], BF16)
        NF = 6
        fpool = [TL(f"fp{i}", [128, T], F32) for i in range(NF)]
        rbm = TL("rbm", [128, 2, 4, 128], F32)
        yT = TL("yT", [128, 8, T], BF16)
        Cf = TL("Cf", [128, 4, 129], F32)
        Cdb = TL("Cdb", [128, 4, 129], BF16)
        stb = [TL(f"stb{i}", [128, 128], BF16) for i in range(4)]
        ktok = [TL(f"ktok{i}", [128, 128], BF16) for i in range(4)]
        hn = TL("hn", [128, 4, 128], F32)
        ytok = [TL(f"ytok{i}", [128, T], BF16) for i in range(2)]
        g_e = TL("g_e", [4, T], F32)
        g_cs = TL("g_cs", [4, T], F32)
        g_gp = TL("g_gp", [4, T], F32)
        lnrep = TL("lnrep", [128, 2, D], F32)
        actT = TL("actT", [128, NFB, T], BF16)
        identf = TL("identf", [128, 128], F32)
        identb = TL("identb", [128, 128], BF16)
        maskS = TL("maskS", [128, 128], F32)
        pv = TL("pv", [128, 196], F32)
        sm = TL("sm", [128, 192], F32)
        grow = TL("grow", [4, 2], F32)
        gsm = TL("gsm", [4, 64], F32)
        wgi = TL("wgi", [128, 8, 4], BF16)
        wgf = TL("wgf", [128, 8, 4], BF16)
        ps = [es.enter_context(nc.psum_tensor(f"ps{i}", [128, 512], F32)) for i in range(8)]
        psbf = [p[:].bitcast(BF16) for p in ps]

        ssx = sm[:, 0:4]; msx = sm[:, 4:8]; rstdx = sm[:, 8:12]
        mhalf = sm[:, 12:28]
        onescol = sm[:, 28:29]
        mhw5 = sm[:, 29:33]
        gtok = sm[:, 33:65]
        decb = sm[:, 65:81]
        cst8 = sm[:, 81:89]
        c4a = sm[:, 89:93]; c4b = sm[:, 93:97]; c4c = sm[:, 97:101]; c4d = sm[:, 101:105]
        absd = sm[:, 105:109]; ddv = sm[:, 109:113]; rec = sm[:, 113:117]
        bst = sm[:, 117:141]
        mv = sm[:, 141:149]
        e4a = sm[:, 149:153]; e4b = sm[:, 153:157]
        decbq = sm[:, 165:181]
        nfb = gsm[:, 0:1]; carryB = gsm[:, 1:2]; cm = gsm[:, 2:6]; Rt = gsm[:, 6:11]; negR = gsm[:, 11:15]
        dR = gsm[:, 15:19]; dec = gsm[:, 19:23]; negbig = gsm[:, 23:27]; Dg = gsm[:, 27:43]
        ones4 = TL("ones4", [4, 128], F32)

        pe = lambda fn, r, w, sig=True: S.op("pe", fn, r, w, signal=sig)
        act = lambda fn, r, w: S.op("act", fn, r, w)
        dve = lambda fn, r, w: S.op("dve", fn, r, w)
        pool = lambda fn, r, w: S.op("pool", fn, r, w)

        free_banks = list(range(8))

        def pb():
            assert free_banks, "PSUM banks exhausted"
            return free_banks.pop(0)

        def pfree(b):
            assert b not in free_banks
            free_banks.append(b)

        fpc = [0]

        def fp():
            i = fpc[0] % NF
            fpc[0] += 1
            return i

        S.op("sp", lambda e: e.dma_start(out=identf[:], in_=cst_d[:, 0:128]), writes=["identf"], dma="c0")
        S.op("sp", lambda e: e.dma_start(out=maskS[:], in_=cst_d[:, 128:256]), writes=["maskS"], dma="c1")
        S.op("sp", lambda e: e.dma_start(out=pv[:], in_=pvec_d[:, :]), writes=["pv"], dma="c2")
        S.op("sp", lambda e: e.dma_start(out=grow[:], in_=grow_d[:, :]), writes=["grow"], dma="c3")
        S.op("sp", lambda e: e.dma_start(out=lnrep[:], in_=lnrep_d[:, :, :]), writes=["lnrep"], dma="c4")
        casts = []
        for c0 in (0, 1540):
            casts.append((sc_in[:, c0:c0 + 1540], w_in_d[:, c0:c0 + 1540], "sc_in"))
        casts.append((sc_out[:, :], w_out_d[:, :], "sc_out"))
        for c0 in (0, 1408):
            casts.append((sc_gate[:, c0:c0 + 1408], w_gate_d[:, c0:c0 + 1408], "sc_gate"))
            casts.append((sc_up[:, c0:c0 + 1408], w_up_d[:, c0:c0 + 1408], "sc_up"))
        for r0 in range(0, DFF, 704):
            casts.append((sc_down[r0:r0 + 704, :], w_down_d[r0:r0 + 704, :], "sc_down"))
        cast_keys = {}
        for i, (o_, i_, nm) in enumerate(casts):
            S.op("pool", lambda e, o_=o_, i_=i_: e.dma_start(out=o_, in_=i_), writes=[(nm, i)], dma=f"cast{i}")
            cast_keys.setdefault(nm, []).append((nm, i))

        dve(lambda e: e.tensor_copy(out=identb[:], in_=identf[:]), ["identf"], ["identb"])
        dve(lambda e: e.memset(mhalf, -0.5), [], ["mhalf"])
        dve(lambda e: e.memset(onescol, 1.0 / 128), [], ["onescol"])
        dve(lambda e: e.memset(ones4[:], 1.0), [], ["ones4"])
        dve(lambda e: e.memset(negbig, -1e30), [], ["negbig"])
        dve(lambda e: e.tensor_scalar(out=mhw5, in0=pv[:, 24:28], scalar1=0.5, scalar2=None, op0=ALU.mult), ["pv"], ["mhw5"])
        dve(lambda e: e.tensor_scalar(out=nfb, in0=grow[:, 1:2], scalar1=-1.0, scalar2=None, op0=ALU.mult), ["grow"], ["nfb"])
        for i in range(32):
            dve(lambda e, i=i: e.tensor_scalar(out=dg4[:, i, :], in0=identb[:], scalar1=pv[:, 40 + i:41 + i], scalar2=None, op0=ALU.mult),
                ["identb", "pv"], ["dg4"] if i in (0, 31) else [])
        S.op("sp", lambda e: e.dma_start(out=wgi[:], in_=sc_in[:, 2048:2052].rearrange("(k p) f -> p k f", p=128)),
             reads=cast_keys["sc_in"], writes=["wg"], dma="c5")
        S.op("sp", lambda e: e.dma_start(out=wgf[:], in_=sc_in[:, 2052:2056].rearrange("(k p) f -> p k f", p=128)),
             reads=cast_keys["sc_in"], writes=["wg"], dma="c6")

        plan = []
        PER = 0
        for g in range(NG):
            pl = []
            for c0 in (0, 512, 1024, 1536, 2056, 2568):
                pl.append((sc_in, 0, 8, c0, 512, "sc_in"))
            for c0 in (0, 512):
                pl.append((sc_out, 0, 8, c0, 512, "sc_out"))
            for p in range(6):
                ncol = 512 if p < 5 else 256
                pl.append((sc_gate, 0, 8, p * 512, ncol, "sc_gate"))
                pl.append((sc_up, 0, 8, p * 512, ncol, "sc_up"))
            for half in range(2):
                for ck in range(3):
                    nk = 8 if ck < 2 else 6
                    pl.append((sc_down, ck * 8, nk, half * 512, 512, "sc_down"))
            PER = len(pl)
            plan += pl

        def wload(i, slot):
            src, k0, nk, c0, ncol, nm = plan[i]
            S.op("sp", lambda e: e.dma_start(out=wring[slot][:, 0:nk, 0:ncol],
                                             in_=src[k0 * 128:(k0 + nk) * 128, c0:c0 + ncol].rearrange("(k p) c -> p k c", p=128)),
                 reads=cast_keys[nm], writes=[("w", slot)], dma=f"w{slot}")

        wr = Ring(R, len(plan), wload)

        def xload(gs, slot):
            g = gs // 4
            j = gs % 4
            s_, m_ = g // nmt, g % nmt
            t0 = m_ * T + j * 128
            S.op("sp", lambda e: e.dma_start(out=xres[slot][:], in_=x_d[s_, t0:t0 + 128, :]), writes=[("x", slot)], dma=f"x{slot}")

        xr = Ring(NX, NG * 4, xload)

        def transposes_to(srcbf, srckey, dst, dstkey, j, scale_cols):
            b = pb()
            for kc in range(8):
                pe(lambda e, kc=kc: e.transpose(out=psbf[b][:, kc * 128:(kc + 1) * 128], in_=srcbf[:, kc * 128:(kc + 1) * 128], identity=identb[:]),
                   [srckey, "identb"], [("ps", b)], sig=(kc == 7))
            dve(lambda e: e.tensor_tensor(out=dst[:, :, j * 128:(j + 1) * 128], in0=psbf[b][:, 0:1024].rearrange("p (k t) -> p k t", k=8),
                                          in1=scale_cols.unsqueeze(2).to_broadcast([128, 8, 128]), op=ALU.mult),
                [("ps", b), "pv"], [(dstkey, j)])
            pfree(b)

        def rstd_from(ss_ap, ms_ap, rs_ap, n, rkeys, key):
            dve(lambda e: e.tensor_scalar(out=ms_ap, in0=ss_ap, scalar1=1.0 / D, scalar2=EPS, op0=ALU.mult, op1=ALU.add), rkeys, [key + "_ms"])
            pool(lambda e: e.tensor_tensor(out=rs_ap, in0=ms_ap, in1=mhalf[:, 0:n], op=ALU.pow), [key + "_ms", "mhalf"], [key])

        XN = [("xnT", j) for j in range(4)]

        dg_res = {0: None, 1: None}

        def interleave(*gens):
            gens = list(gens)
            while gens:
                for gq in list(gens):
                    try:
                        next(gq)
                    except StopIteration:
                        gens.remove(gq)

        def stageA_front(g):
            for j in range(4):
                gs = 4 * g + j
                sl = xr.need(gs)
                act(lambda e, sl=sl, j=j: e.activation(out=junk[:], in_=xres[sl][:], func=AF.Square, accum_out=ssx[:, j:j + 1]),
                    [("x", sl)], [("ssx", j)])
            rstd_from(ssx, msx, rstdx, 4, [("ssx", j) for j in range(4)], "rstdx")

        def stageA_mid(g):
            for j in range(4):
                gs = 4 * g + j
                sl = gs % NX
                r = j % 2
                act(lambda e, sl=sl, j=j, r=r: e.activation(out=xs[r][:], in_=xres[sl][:], func=AF.Copy, scale=rstdx[:, j:j + 1]),
                    [("x", sl), "rstdx"], [("xs", r)])
                transposes_to(xs[r], ("xs", r), xnT, "xnT", j, pv[:, 0:8])

        def gates_front(g):
            first = (g % nmt == 0)
            bi = pb(); bf_ = pb()
            for (bk, wt) in ((bi, wgi), (bf_, wgf)):
                for kc in range(8):
                    pe(lambda e, bk=bk, wt=wt, kc=kc: e.matmul(ps[bk][0:4, :], lhsT=wt[:, kc, :], rhs=xnT[:, kc, :], start=(kc == 0), stop=(kc == 7)),
                       XN + ["wg"], [("ps", bk)], sig=(kc == 7))
            act(lambda e: e.activation(out=g_e[:], in_=ps[bf_][0:4, :], func=AF.Exp, bias=nfb, scale=-1.0), [("ps", bf_), "nfb"], ["g_e"])
            pfree(bf_)
            act(lambda e: e.activation(out=g_e[:], in_=g_e[:], func=AF.Ln, bias=1.0), ["g_e"], ["g_e"])
            onesb = ones4[:, 0:1].to_broadcast([4, T])
            if first:
                dve(lambda e: e.tensor_tensor_scan(out=g_cs[:], data0=onesb, data1=g_e[:], initial=0.0, op0=ALU.mult, op1=ALU.add),
                    ["g_e", "ones4"], ["g_cs"])
            else:
                dve(lambda e: e.tensor_tensor_scan(out=g_cs[:], data0=onesb, data1=g_e[:], initial=carryB, op0=ALU.mult, op1=ALU.add),
                    ["g_e", "ones4", "carryB"], ["g_cs"])
            dve(lambda e: e.tensor_copy(out=carryB, in_=g_cs[:, T - 1:T]), ["g_cs"], ["carryB"])
            dve(lambda e: e.scalar_tensor_tensor(out=g_gp[:], in0=ps[bi][0:4, :], scalar=grow[:, 0:1], in1=g_cs[:], op0=ALU.add, op1=ALU.add),
                [("ps", bi), "grow", "g_cs"], ["g_gp"])
            pfree(bi)
            dve(lambda e: e.tensor_reduce(out=cm, in_=g_gp[:].rearrange("p (c l) -> p c l", l=128), axis=AX.X, op=ALU.max), ["g_gp"], ["cm"])
            if first:
                dve(lambda e: e.memset(Rt[:, 0:1], 0.0), [], ["Rt"])
            else:
                dve(lambda e: e.tensor_copy(out=Rt[:, 0:1], in_=Rt[:, 4:5]), ["Rt"], ["Rt"])
            dve(lambda e: e.tensor_tensor_scan(out=Rt[:, 1:5], data0=negbig, data1=cm, initial=Rt[:, 0:1], op0=ALU.max, op1=ALU.max),
                ["Rt", "cm", "negbig"], ["Rt"])
            dve(lambda e: e.tensor_scalar(out=negR, in0=Rt[:, 1:5], scalar1=-1.0, scalar2=None, op0=ALU.mult), ["Rt"], ["negR"])
            dve(lambda e: e.tensor_tensor(out=dR, in0=Rt[:, 0:4], in1=Rt[:, 1:5], op=ALU.subtract), ["Rt"], ["dR"])
            act(lambda e: e.activation(out=dec, in_=dR, func=AF.Exp), ["dR"], ["dec"])
            for c in range(4):
                act(lambda e, c=c: e.activation(out=g_gp[:, c * 128:(c + 1) * 128], in_=g_gp[:, c * 128:(c + 1) * 128], func=AF.Exp, bias=negR[:, c:c + 1]),
                    ["g_gp", "negR", "cm"], ["g_gp"])
                act(lambda e, c=c: e.activation(out=g_cs[:, c * 128:(c + 1) * 128], in_=g_cs[:, c * 128:(c + 1) * 128], func=AF.Exp, bias=negR[:, c:c + 1]),
                    ["g_cs", "negR", "carryB"], ["g_cs"])
            dve(lambda e: e.tensor_tensor(out=Dg.rearrange("p (c h) -> p c h", h=4), in0=dec.unsqueeze(2).to_broadcast([4, 4, 4]),
                                          in1=identf[0:4, 0:4].unsqueeze(1).to_broadcast([4, 4, 4]), op=ALU.mult), ["dec", "identf"], ["Dg"])

        def macro(g):
            s_, m_ = g // nmt, g % nmt
            first = (m_ == 0)
            wb = g * PER
            if g == 0:
                stageA_front(0)
                stageA_mid(0)
                gates_front(0)
            def qk_proj(blk):
                wi = wb + (0 if blk < 4 else 1)
                wsl = wr.need(wi)
                b4 = blk % 4
                b = pb()
                for kc in range(8):
                    pe(lambda e, kc=kc: e.matmul(ps[b][:, :], lhsT=wring[wsl][:, kc, b4 * 128:(b4 + 1) * 128], rhs=xnT[:, kc, :],
                                                 start=(kc == 0), stop=(kc == 7)),
                       XN + [("w", wsl)], [("ps", b)], sig=(kc == 7))
                if blk in (3, 7):
                    wr.done(wi)
                r = blk % 2
                stt = qkst[r]
                if first:
                    dve(lambda e: e.memset(stt[:, 0:3], 0.0), [], [("qkst", r)])
                else:
                    dve(lambda e: e.tensor_copy(out=stt[:, 0:3], in_=qhalo[:, blk, :]), [("qhalo", blk)], [("qkst", r)])
                act(lambda e: e.activation(out=stt[:, 3:T + 3], in_=ps[b][:, :], func=AF.Copy), [("ps", b)], [("qkst", r)])
                pfree(b)
                dve(lambda e: e.tensor_copy(out=qhalo[:, blk, :], in_=stt[:, T:T + 3]), [("qkst", r)], [("qhalo", blk)])

            def qk_conv(blk):
                r = blk % 2
                stt = qkst[r]
                b2 = pb()
                for k in range(4):
                    pe(lambda e, k=k: e.matmul(ps[b2][:, :], lhsT=dg4[:, k * 8 + blk, :], rhs=stt[:, k:k + T], start=(k == 0), stop=(k == 3)),
                       [("qkst", r), "dg4"], [("ps", b2)], sig=(k == 3))
                act(lambda e: e.activation(out=qkT[:, blk, :], in_=ps[b2][:, :], func=AF.Silu, bias=pv[:, 16 + blk:17 + blk]),
                    [("ps", b2), "pv"], [("qkT", blk)])
                pfree(b2)

            for blk in range(8):
                qk_proj(blk)
                if blk > 0:
                    qk_conv(blk - 1)
            qk_conv(7)
            wi_v = wb + 2
            wsl_v = wr.need(wi_v)
            wi_o = wb + 3
            wsl_o = wr.need(wi_o)

            def tokmajor_proj(wsl, j):
                b = pb()
                for kc in range(8):
                    pe(lambda e, kc=kc: e.matmul(ps[b][:, :], lhsT=xnT[:, kc, j * 128:(j + 1) * 128], rhs=wring[wsl][:, kc, :],
                                                 start=(kc == 0), stop=(kc == 7)),
                       [("xnT", j), ("w", wsl)], [("ps", b)], sig=(kc == 7))
                return b

            bt = pb()
            for c in range(4):
                pe(lambda e, c=c: e.transpose(out=ps[bt][:, c * 8:c * 8 + 4], in_=g_gp[0:4, c * 128:(c + 1) * 128], identity=identf[0:4, 0:4]),
                   ["g_gp", "identf"], [("ps", bt)], sig=False)
                pe(lambda e, c=c: e.transpose(out=ps[bt][:, c * 8 + 4:c * 8 + 8], in_=g_cs[0:4, c * 128:(c + 1) * 128], identity=identf[0:4, 0:4]),
                   ["g_cs", "identf"], [("ps", bt)], sig=(c == 3))
            dve(lambda e: e.tensor_copy(out=gtok, in_=ps[bt][:, 0:32]), [("ps", bt)], ["gtok"])
            pfree(bt)
            bd = pb()
            pe(lambda e: e.matmul(ps[bd][:, 0:16], lhsT=ones4[0:4, :], rhs=Dg, start=True, stop=True), ["ones4", "Dg"], [("ps", bd)])
            dve(lambda e: e.tensor_copy(out=decb, in_=ps[bd][:, 0:16]), [("ps", bd)], ["decb"])
            pfree(bd)
            dve(lambda e: e.tensor_scalar(out=decbq, in0=decb, scalar1=QSCALE, scalar2=None, op0=ALU.mult), ["decb"], ["decbq"])
            for j in range(4):
                b = tokmajor_proj(wsl_v, j)
                dve(lambda e, b=b, j=j: e.tensor_tensor(out=vp[:, j, :, 0:128], in0=ps[b][:, :].rearrange("p (h e) -> p h e", h=4),
                                                        in1=gtok[:, j * 8:j * 8 + 4].unsqueeze(2).to_broadcast([128, 4, 128]), op=ALU.mult),
                    [("ps", b), "gtok"], [("vp", j)])
                dve(lambda e, j=j: e.tensor_copy(out=vp[:, j, :, 128:129], in_=gtok[:, j * 8:j * 8 + 4].unsqueeze(2)), ["gtok"], [("vp", j)])
                pfree(b)
            wr.done(wi_v)
            for j in range(4):
                b = tokmajor_proj(wsl_o, j)
                act(lambda e, b=b, j=j: e.activation(out=tho[:, j, :], in_=ps[b][:, :], func=AF.Tanh, scale=0.5), [("ps", b)], [("tho", j)])
                pfree(b)
            wr.done(wi_o)

            wcv = wr.need(wb + 4)
            wcg = wr.need(wb + 5)

            def conv_gen():
                st = {}

                perm = [0, 1, 2, 3] if g % 2 == 0 else [3, 2, 1, 0]

                def c1(i):
                    b4 = perm[i]
                    r = b4 % 2
                    dgt = dg31[r]
                    if dg_res[r] != b4:
                        dg_res[r] = b4
                        for k in range(31):
                            pool(lambda e, k=k: e.tensor_scalar(out=dgt[:, k, :], in0=identb[:], scalar1=pv[:, 72 + b4 * 31 + k:73 + b4 * 31 + k],
                                                                scalar2=1.0, op0=ALU.mult, op1=ALU.mult),
                                 ["identb", "pv"], [("dg31", r)] if k in (0, 30) else [])
                    bg = pb()
                    for kc in range(8):
                        pe(lambda e, kc=kc: e.matmul(ps[bg][:, :], lhsT=wring[wcg][:, kc, b4 * 128:(b4 + 1) * 128], rhs=xnT[:, kc, :],
                                                     start=(kc == 0), stop=(kc == 7)), XN + [("w", wcg)], [("ps", bg)], sig=(kc == 7))
                    bv = pb()
                    for kc in range(8):
                        pe(lambda e, kc=kc: e.matmul(ps[bv][:, :], lhsT=wring[wcv][:, kc, b4 * 128:(b4 + 1) * 128], rhs=xnT[:, kc, :],
                                                     start=(kc == 0), stop=(kc == 7)), XN + [("w", wcv)], [("ps", bv)], sig=(kc == 7))
                    if i == 3:
                        wr.done(wb + 4)
                        wr.done(wb + 5)
                    f1 = fp()
                    act(lambda e: e.activation(out=fpool[f1][:], in_=ps[bg][:, :], func=AF.Tanh, scale=0.5), [("ps", bg)], [("f", f1)])
                    pfree(bg)
                    ru = i % 2
                    ut = ust[ru]
                    if first:
                        dve(lambda e: e.memset(ut[:, 0:30], 0.0), [], [("ust", ru)])
                    else:
                        dve(lambda e: e.tensor_copy(out=ut[:, 0:30], in_=uhalo[:, b4, :]), [("uhalo", b4)], [("ust", ru)])
                    dve(lambda e: e.scalar_tensor_tensor(out=ut[:, 30:T + 30], in0=fpool[f1][:], scalar=1.0, in1=ps[bv][:, :],
                                                         op0=ALU.add, op1=ALU.mult),
                        [("f", f1), ("ps", bv)], [("ust", ru)])
                    pfree(bv)
                    dve(lambda e: e.tensor_copy(out=uhalo[:, b4, :], in_=ut[:, T:T + 30]), [("ust", ru)], [("uhalo", b4)])

                def c2(i):
                    b4 = perm[i]
                    r = b4 % 2
                    ru = i % 2
                    dgt = dg31[r]
                    ut = ust[ru]
                    bc = pb()
                    for k in range(31):
                        pe(lambda e, k=k: e.matmul(ps[bc][:, :], lhsT=dgt[:, k, :], rhs=ut[:, k:k + T], start=(k == 0), stop=(k == 30)),
                           [("ust", ru), ("dg31", r)], [("ps", bc)], sig=(k == 30))
                    fu = fp()
                    fq = fp()
                    st[b4] = (fu, fq)
                    act(lambda e: e.activation(out=fpool[fu][:], in_=ps[bc][:, :], func=AF.Identity, bias=pv[:, 28 + b4:29 + b4], scale=0.5),
                        [("ps", bc), "pv"], [("f", fu)])
                    pfree(bc)
                    act(lambda e: e.activation(out=fpool[fq][:], in_=fpool[fu][:], func=AF.Square), [("f", fu)], [("f", fq)])

                def c3(i):
                    b4 = perm[i]
                    fu, fq = st[b4]
                    bs = pb()
                    for j in range(4):
                        pe(lambda e, j=j: e.matmul(ps[bs][:, 2 * j:2 * j + 1], lhsT=fpool[fu][:, j * 128:(j + 1) * 128], rhs=onescol, start=True, stop=True),
                           [("f", fu), "onescol"], [("ps", bs)], sig=False)
                        pe(lambda e, j=j: e.matmul(ps[bs][:, 2 * j + 1:2 * j + 2], lhsT=fpool[fq][:, j * 128:(j + 1) * 128], rhs=onescol, start=True, stop=True),
                           [("f", fq), "onescol"], [("ps", bs)], sig=(j == 3))
                    dve(lambda e: e.tensor_copy(out=cst8, in_=ps[bs][:, 0:8]), [("ps", bs)], ["cst8"])
                    pfree(bs)
                    cv_ = cst8.rearrange("p (j q) -> p j q", q=2)
                    dve(lambda e: e.tensor_tensor(out=c4a, in0=cv_[:, :, 0], in1=cv_[:, :, 0], op=ALU.mult), ["cst8"], ["c4a"])
                    dve(lambda e: e.scalar_tensor_tensor(out=c4b, in0=c4a, scalar=-1.0, in1=cv_[:, :, 1], op0=ALU.mult, op1=ALU.add), ["c4a", "cst8"], ["c4b"])
                    dve(lambda e: e.tensor_scalar(out=c4b, in0=c4b, scalar1=EPS, scalar2=None, op0=ALU.add), ["c4b"], ["c4b"])
                    pool(lambda e: e.tensor_tensor(out=c4c, in0=c4b, in1=mhalf[:, 0:4], op=ALU.pow), ["c4b", "mhalf"], ["c4c"])
                    dve(lambda e: e.scalar_tensor_tensor(out=c4d, in0=cv_[:, :, 0], scalar=-1.0, in1=c4c, op0=ALU.mult, op1=ALU.mult), ["cst8", "c4c"], ["c4d"])
                    dve(lambda e: e.tensor_copy(out=rbm[:, 0, :, :], in_=c4c.unsqueeze(2).to_broadcast([128, 4, 128])), ["c4c"], ["rbm0"])
                    dve(lambda e: e.tensor_copy(out=rbm[:, 1, :, :], in_=c4d.unsqueeze(2).to_broadcast([128, 4, 128])), ["c4d"], ["rbm1"])

                def c4(i):
                    b4 = perm[i]
                    fu, fq = st[b4]
                    br = pb(); bn = pb()
                    for j in range(4):
                        pe(lambda e, j=j: e.transpose(out=ps[br][:, j * 128:(j + 1) * 128], in_=rbm[:, 0, j, :], identity=identf[:]),
                           ["rbm0", "identf"], [("ps", br)], sig=(j == 3))
                    for j in range(4):
                        pe(lambda e, j=j: e.transpose(out=ps[bn][:, j * 128:(j + 1) * 128], in_=rbm[:, 1, j, :], identity=identf[:]),
                           ["rbm1", "identf"], [("ps", bn)], sig=(j == 3))
                    dve(lambda e: e.tensor_tensor(out=fpool[fu][:], in0=fpool[fu][:], in1=ps[br][:, :], op=ALU.mult), [("f", fu), ("ps", br)], [("f", fu)])
                    dve(lambda e: e.tensor_tensor(out=fpool[fu][:], in0=fpool[fu][:], in1=ps[bn][:, :], op=ALU.add), [("f", fu), ("ps", bn)], [("f", fu)])
                    pfree(br)
                    pfree(bn)
                    act(lambda e: e.activation(out=yT[:, 4 + b4, :], in_=fpool[fu][:], func=AF.Silu, bias=pv[:, 36 + b4:37 + b4], scale=pv[:, 32 + b4:33 + b4]),
                        [("f", fu), "pv"], [("yTc", b4)])

                order = [[(c1, 0)], [(c1, 1), (c2, 0)], [(c3, 0), (c2, 1)], [(c4, 0), (c1, 2)], [(c3, 1), (c2, 2)],
                         [(c4, 1), (c1, 3)], [(c3, 2), (c2, 3)], [(c4, 2)], [(c3, 3)], [(c4, 3)]]
                for grp in order:
                    for fn_, b4 in grp:
                        fn_(b4)
                    yield

            def mlstm_gen():
                if first:
                    dve(lambda e: e.memset(Cf[:], 0.0), [], [("Cf", h) for h in range(4)])
                st = {}

                def pa(j):
                    tk = slice(j * 128, (j + 1) * 128)
                    for h in range(4):
                        ci = j * 4 + h
                        act(lambda e, h=h, ci=ci: e.activation(out=Cdb[:, h, :], in_=Cf[:, h, :], func=AF.Copy, scale=decbq[:, ci:ci + 1]),
                            [("Cf", h), "decbq"], [("Cdb", h)])
                    bS = pb(); bK = pb()
                    st[j] = (bS, bK)
                    for h in range(4):
                        pe(lambda e, h=h: e.matmul(ps[bS][:, h * 128:(h + 1) * 128], lhsT=qkT[:, 4 + h, tk], rhs=qkT[:, h, tk], start=True, stop=True),
                           [("qkT", 4 + h), ("qkT", h)], [("ps", bS)], sig=(h == 3))
                    for h in range(4):
                        pe(lambda e, h=h: e.transpose(out=psbf[bK][:, h * 128:(h + 1) * 128], in_=qkT[:, 4 + h, tk], identity=identb[:]),
                           [("qkT", 4 + h), "identb"], [("ps", bK)], sig=(h == 3))

                def pbb(j):
                    tk = slice(j * 128, (j + 1) * 128)
                    bS, bK = st[j]
                    for h in range(4):
                        dve(lambda e, h=h: e.tensor_tensor(out=stb[h][:], in0=ps[bS][:, h * 128:(h + 1) * 128], in1=maskS[:], op=ALU.mult),
                            [("ps", bS), "maskS"], [("stb", h)])
                        act(lambda e, h=h: e.activation(out=ktok[h][:], in_=psbf[bK][:, h * 128:(h + 1) * 128], func=AF.Copy), [("ps", bK)], [("ktok", h)])
                    pfree(bS)
                    pfree(bK)
                    bN = [pb(), pb()]
                    bV = [pb(), pb()]
                    for h in range(4):
                        o_ = ps[bN[h // 2]][:, (h % 2) * 129:(h % 2) * 129 + 129]
                        pe(lambda e, h=h, o_=o_: e.matmul(o_, lhsT=qkT[:, h, tk], rhs=Cdb[:, h, :], start=True, stop=False),
                           [("qkT", h), ("Cdb", h)], [("ps", bN[h // 2])], sig=False)
                        pe(lambda e, h=h, o_=o_: e.matmul(o_, lhsT=stb[h][:], rhs=vp[:, j, h, :], start=False, stop=True),
                           [("stb", h), ("vp", j)], [("ps", bN[h // 2])], sig=(h % 2 == 1))
                    for h in range(4):
                        o_ = ps[bV[h // 2]][:, (h % 2) * 129:(h % 2) * 129 + 129]
                        pe(lambda e, h=h, o_=o_: e.matmul(o_, lhsT=ktok[h][:], rhs=vp[:, j, h, :], start=True, stop=True),
                           [("ktok", h), ("vp", j)], [("ps", bV[h // 2])], sig=(h % 2 == 1))
                    for h in range(4):
                        o_ = ps[bV[h // 2]][:, (h % 2) * 129:(h % 2) * 129 + 129]
                        ci = j * 4 + h
                        dve(lambda e, h=h, o_=o_, ci=ci: e.scalar_tensor_tensor(out=Cf[:, h, :], in0=Cf[:, h, :], scalar=decb[:, ci:ci + 1], in1=o_,
                                                                            op0=ALU.mult, op1=ALU.add),
                            [("ps", bV[h // 2]), ("Cf", h), "decb"], [("Cf", h)])
                    pfree(bV[0])
                    pfree(bV[1])
                    for i in range(2):
                        dcol = ps[bN[i]][:, 0:258].rearrange("p (h e) -> p h e", e=129)[:, :, 128]
                        act(lambda e, i=i, dcol=dcol: e.activation(out=absd[:, 2 * i:2 * i + 2], in_=dcol, func=AF.Abs),
                            [("ps", bN[i])], [("absd", i)])
                    dve(lambda e: e.tensor_tensor(out=ddv, in0=absd, in1=gtok[:, j * 8 + 4:j * 8 + 8], op=ALU.max),
                        [("absd", 0), ("absd", 1), "gtok"], ["ddv"])
                    dve(lambda e: e.reciprocal(out=rec, in_=ddv), ["ddv"], ["rec"])
                    for h in range(4):
                        nb_ = ps[bN[h // 2]]
                        c0 = (h % 2) * 129
                        act(lambda e, h=h, nb_=nb_, c0=c0: e.activation(out=hn[:, h, :], in_=nb_[:, c0:c0 + 128], func=AF.Copy, scale=rec[:, h:h + 1]),
                            [("ps", bN[h // 2]), "rec"], [("hn", h)])
                        dve(lambda e, h=h: e.bn_stats(out=bst[:, h * 6:(h + 1) * 6], in_=hn[:, h, :]), [("hn", h)], [("bst", h)])
                        dve(lambda e, h=h: e.bn_aggr(out=mv[:, h * 2:(h + 1) * 2], in_=bst[:, h * 6:(h + 1) * 6]), [("bst", h)], [("mv", h)])
                    pfree(bN[0])
                    pfree(bN[1])
                    mvv = mv.rearrange("p (h q) -> p h q", q=2)
                    MV = [("mv", h) for h in range(4)]
                    dve(lambda e: e.tensor_scalar(out=e4a, in0=mvv[:, :, 1], scalar1=EPS, scalar2=None, op0=ALU.add), MV, ["e4a"])
                    pool(lambda e: e.tensor_tensor(out=e4b, in0=e4a, in1=mhalf[:, 0:4], op=ALU.pow), ["e4a", "mhalf"], ["e4b"])
                    dve(lambda e: e.scalar_tensor_tensor(out=e4a, in0=mvv[:, :, 0], scalar=-1.0, in1=e4b, op0=ALU.mult, op1=ALU.mult), MV + ["e4b"], ["e4a"])
                    for h in range(4):
                        act(lambda e, h=h: e.activation(out=hn[:, h, :], in_=hn[:, h, :], func=AF.Identity, bias=e4a[:, h:h + 1], scale=e4b[:, h:h + 1]),
                            [("hn", h), "e4a", "e4b"], [("hn", h)])
                    r = j % 2
                    dve(lambda e: e.scalar_tensor_tensor(out=ytok[r][:], in0=tho[:, j, :], scalar=1.0, in1=hn[:].rearrange("p h e -> p (h e)"),
                                                         op0=ALU.add, op1=ALU.mult),
                        [("tho", j)] + [("hn", h) for h in range(4)], [("ytok", r)])

                def pc(j):
                    tk = slice(j * 128, (j + 1) * 128)
                    r = j % 2
                    b5 = pb()
                    for h in range(4):
                        pe(lambda e, h=h: e.transpose(out=psbf[b5][:, h * 128:(h + 1) * 128], in_=ytok[r][:, h * 128:(h + 1) * 128], identity=identb[:]),
                           [("ytok", r), "identb"], [("ps", b5)], sig=(h == 3))
                    dve(lambda e: e.tensor_tensor(out=yT[:, 0:4, tk], in0=psbf[b5][:, 0:512].rearrange("p (h t) -> p h t", h=4),
                                                  in1=mhw5.unsqueeze(2).to_broadcast([128, 4, 128]), op=ALU.mult),
                        [("ps", b5), "mhw5"], [("yTm", j)])
                    pfree(b5)

                order = [[(pa, 0)], [(pbb, 0)], [(pa, 1)], [(pbb, 1)], [(pc, 0), (pa, 2)], [(pbb, 2)], [(pc, 1), (pa, 3)], [(pbb, 3)], [(pc, 2)], [(pc, 3)]]
                for grp in order:
                    for fn_, j in grp:
                        fn_(j)
                    yield

            cg_ = conv_gen()
            next(cg_)
            next(cg_)
            mg_ = mlstm_gen()

            wo0 = wr.need(wb + 6)
            wo1 = wr.need(wb + 7)
            YC = [("yTc", b) for b in range(4)]

            def epi_ss(bk, col, key):
                act(lambda e: e.activation(out=junk[:, 0:512], in_=ps[bk][:, :], func=AF.Square, accum_out=sm[:, col:col + 1]), [("ps", bk)], [key])

            def epi_fin(bk, lni, half, sl):
                hs = slice(half * 512, (half + 1) * 512)
                dve(lambda e: e.tensor_tensor(out=ps[bk][:, :], in0=ps[bk][:, :], in1=lnrep[:, lni, hs], op=ALU.mult), [("ps", bk), "lnrep"], [("ps", bk)])
                dve(lambda e: e.scalar_tensor_tensor(out=xres[sl][:, hs], in0=ps[bk][:, :], scalar=sm[:, 161:162], in1=xres[sl][:, hs], op0=ALU.mult, op1=ALU.add),
                    [("ps", bk), "ep_rs", ("x", sl)], [("x", sl)])
                pfree(bk)

            def epilogue(j, ba, bb, lni, sl):
                epi_ss(ba, 157, "ep_s0")
                epi_ss(bb, 158, "ep_s1")
                dve(lambda e: e.tensor_tensor(out=sm[:, 159:160], in0=sm[:, 157:158], in1=sm[:, 158:159], op=ALU.add), ["ep_s0", "ep_s1"], ["ep_ss"])
                rstd_from(sm[:, 159:160], sm[:, 160:161], sm[:, 161:162], 1, ["ep_ss"], "ep_rs")
                epi_fin(ba, lni, 0, sl)
                epi_fin(bb, lni, 1, sl)

            dstage = {}

            def epilogue_d_early(j, b0):
                act(lambda e: e.activation(out=junk[:, 0:512], in_=ps[b0][:, :], func=AF.Square, accum_out=sm[:, 181 + j:182 + j]), [("ps", b0)], [("ds0", j)])
                f = fp()
                dstage[j] = f
                dve(lambda e: e.tensor_tensor(out=fpool[f][:], in0=ps[b0][:, :], in1=lnrep[:, 1, 0:512], op=ALU.mult), [("ps", b0), "lnrep"], [("f", f)])
                pfree(b0)

            def epilogue_d_late(j, b1, sl):
                f = dstage[j]
                epi_ss(b1, 158, "ep_s1")
                dve(lambda e: e.tensor_tensor(out=sm[:, 159:160], in0=sm[:, 181 + j:182 + j], in1=sm[:, 158:159], op=ALU.add), [("ds0", j), "ep_s1"], ["ep_ss"])
                rstd_from(sm[:, 159:160], sm[:, 160:161], sm[:, 161:162], 1, ["ep_ss"], "ep_rs")
                dve(lambda e: e.scalar_tensor_tensor(out=xres[sl][:, 0:512], in0=fpool[f][:], scalar=sm[:, 161:162], in1=xres[sl][:, 0:512], op0=ALU.mult, op1=ALU.add),
                    [("f", f), "ep_rs", ("x", sl)], [("x", sl)])
                epi_fin(b1, 1, 1, sl)

            def wstage(j):
                gs = 4 * g + j
                sl = gs % NX
                tk = slice(j * 128, (j + 1) * 128)
                ba = pb(); bb = pb()
                for (bk, wsl) in ((ba, wo0), (bb, wo1)):
                    for kc in range(8):
                        pe(lambda e, bk=bk, wsl=wsl, kc=kc: e.matmul(ps[bk][:, :], lhsT=yT[:, kc, tk], rhs=wring[wsl][:, kc, :], start=(kc == 0), stop=(kc == 7)),
                           [("yTm", j), ("w", wsl)] + YC, [("ps", bk)], sig=(kc == 7))
                if j == 3:
                    wr.done(wb + 6)
                    wr.done(wb + 7)
                epilogue(j, ba, bb, 0, sl)
                act(lambda e: e.activation(out=junk[:], in_=xres[sl][:], func=AF.Square, accum_out=sm[:, 164:165]), [("x", sl)], ["hss"])
                rstd_from(sm[:, 164:165], sm[:, 162:163], sm[:, 163:164], 1, ["hss"], "hrs")
                r = j % 2
                act(lambda e: e.activation(out=xs[r][:], in_=xres[sl][:], func=AF.Copy, scale=sm[:, 163:164]), [("x", sl), "hrs"], [("xs", r)])

            def tstage(j):
                r = j % 2
                transposes_to(xs[r], ("xs", r), hnT, "xnT", j, pv[:, 8:16])

            M = lambda: next(mg_, None)
            C = lambda: next(cg_, None)
            M(); C(); C(); M(); C(); M(); C(); C(); M(); C(); M(); C(); C(); M()
            assert next(cg_, "end") == "end"
            wstage(0); M(); wstage(1); tstage(0); M(); M(); wstage(2); tstage(1); M()
            assert next(mg_, "end") == "end"
            wstage(3); tstage(2); tstage(3)

            for p in range(6):
                wg_ = wr.need(wb + 8 + 2 * p)
                wu_ = wr.need(wb + 9 + 2 * p)
                nb = 4 if p < 5 else 2
                for bq in range(nb):
                    fb = p * 4 + bq
                    bg = pb()
                    for kc in range(8):
                        pe(lambda e, bg=bg, kc=kc, bq=bq, wg_=wg_: e.matmul(ps[bg][:, :], lhsT=wring[wg_][:, kc, bq * 128:(bq + 1) * 128], rhs=hnT[:, kc, :],
                                                                           start=(kc == 0), stop=(kc == 7)), XN + [("w", wg_)], [("ps", bg)], sig=(kc == 7))
                    bu = pb()
                    for kc in range(8):
                        pe(lambda e, bu=bu, kc=kc, bq=bq, wu_=wu_: e.matmul(ps[bu][:, :], lhsT=wring[wu_][:, kc, bq * 128:(bq + 1) * 128], rhs=hnT[:, kc, :],
                                                                           start=(kc == 0), stop=(kc == 7)), XN + [("w", wu_)], [("ps", bu)], sig=(kc == 7))
                    f1 = fp()
                    act(lambda e, f1=f1, bg=bg: e.activation(out=fpool[f1][:], in_=ps[bg][:, :], func=AF.Silu), [("ps", bg)], [("f", f1)])
                    pfree(bg)
                    dve(lambda e, f1=f1, bu=bu, fb=fb: e.tensor_tensor(out=actT[:, fb, :], in0=fpool[f1][:], in1=ps[bu][:, :], op=ALU.mult),
                        [("f", f1), ("ps", bu)], [("actT", fb)])
                    pfree(bu)
                wr.done(wb + 8 + 2 * p)
                wr.done(wb + 9 + 2 * p)

            AT = [("actT", fb) for fb in range(NFB)]
            banks = {}
            nxt = g + 1 if g + 1 < NG else None
            if nxt is not None:
                stageA_front(nxt)
            for half in range(2):
                for j in range(4):
                    banks[(j, half)] = pb()
                for ck in range(3):
                    wi = wb + 20 + half * 3 + ck
                    wsl = wr.need(wi)
                    nk = 8 if ck < 2 else 6
                    for j in range(4):
                        tk = slice(j * 128, (j + 1) * 128)
                        bk = banks[(j, half)]
                        for q in range(nk):
                            fc = ck * 8 + q
                            pe(lambda e, bk=bk, fc=fc, q=q, wsl=wsl, tk=tk: e.matmul(ps[bk][:, :], lhsT=actT[:, fc, tk], rhs=wring[wsl][:, q, :],
                                                                                 start=(fc == 0), stop=(fc == NFB - 1)),
                               [("actT", fc), ("w", wsl)], [("ps", bk)], sig=(q == nk - 1))
                    wr.done(wi)
                if half == 0:
                    for j in range(4):
                        epilogue_d_early(j, banks[(j, 0)])
                    if nxt is not None:
                        stageA_mid(nxt)
                if half == 1:
                    for j in range(4):
                        gs = 4 * g + j
                        sl = gs % NX
                        epilogue_d_late(j, banks[(j, 1)], sl)
                        t0 = m_ * T + j * 128
                        S.op("pool", lambda e, sl=sl, t0=t0: e.dma_start(out=out_d[s_, t0:t0 + 128, :], in_=xres[sl][:]), reads=[("x", sl)], dma=f"st{sl}")
                        xr.done(gs)
            if nxt is not None:
                gates_front(nxt)

        for g in range(NG):
            macro(g)
        S.final_wait("pool")
        S.emit()
    return nc


def host_prep(inputs):
    f = np.float32
    pvec = np.zeros((128, 196), f)
    pvec[:, 0:8] = np.asarray(inputs["ln_mix_pre"], f)[0].reshape(8, 128).T
    pvec[:, 8:16] = np.asarray(inputs["ln_ffn_pre"], f)[0].reshape(8, 128).T
    pvec[:, 16:24] = np.asarray(inputs["qk_conv_b"], f)[0].reshape(8, 128).T
    pvec[:, 24:28] = np.asarray(inputs["mh_norm_w"], f)[0].reshape(4, 128).T
    pvec[:, 28:32] = np.asarray(inputs["dw_conv_b"], f)[0].reshape(4, 128).T
    pvec[:, 32:36] = np.asarray(inputs["conv_norm_w"], f)[0].reshape(4, 128).T
    pvec[:, 36:40] = np.asarray(inputs["conv_norm_b"], f)[0].reshape(4, 128).T
    pvec[:, 40:72] = np.asarray(inputs["qk_conv_w"], f)[0].reshape(4, 8, 128).transpose(2, 0, 1).reshape(128, 32)
    pvec[:, 72:196] = np.asarray(inputs["dw_conv_w"], f)[0].reshape(31, 4, 128).transpose(2, 1, 0).reshape(128, 124)
    grow = np.stack([np.asarray(inputs["i_bias"], f)[0], np.asarray(inputs["f_bias"], f)[0]], axis=1).astype(f)
    lnrep = np.ascontiguousarray(np.broadcast_to(
        np.stack([np.asarray(inputs["ln_mix_post"], f)[0], np.asarray(inputs["ln_ffn_post"], f)[0]], axis=0)[None], (128, 2, D))).astype(f)
    cst = np.zeros((128, 256), f)
    cst[:, 0:128] = np.eye(128, dtype=f)
    si = np.arange(128)
    cst[:, 128:256] = np.where(si[:, None] <= si[None, :], f(QSCALE), f(0.0))
    return dict(pvec=pvec, grow=grow, lnrep=lnrep, cst=cst)


def run(inputs, nseq, nmt, ncores):
    x = np.asarray(inputs["x"], np.float32)
    common = host_prep(inputs)
    for k in ("w_in", "w_out", "w_gate", "w_up", "w_down"):
        common[k] = np.ascontiguousarray(np.asarray(inputs[k], np.float32))
    nc = build(nseq, nmt)
    in_maps = []
    for c in range(ncores):
        m = dict(common)
        m["x"] = np.ascontiguousarray(x[c * nseq:(c + 1) * nseq])
        in_maps.append(m)
    res = run_bass_kernel_spmd(nc, in_maps, core_ids=list(range(ncores)), **RUN_KW)
    return np.concatenate([np.asarray(r["out"]) for r in res.results], axis=0).astype(np.float32)


def kernel(**inputs):
    return run(inputs, 2, 8, NCORES)
```
